# Optimizing a Trainium2 kernel written in Bass

```python
import jax, jax.numpy as jnp
from jax import lax
import numpy as np

D_MODEL = 1024
BATCH = 8
SEQ = 2048
DEPTH = 4

CHUNK = 64
N_MIXERS = 2
N_ATTN_LAYERS = (DEPTH + 1) // 2
N_REC_LAYERS = DEPTH // 2

N_HEADS = 16
HEAD_DIM = D_MODEL // N_HEADS
LEFT_CHUNKS = 8
BAND_CHUNKS = LEFT_CHUNKS + 1
BAND = BAND_CHUNKS * CHUNK
MAX_REL_DIST = 128
N_REL = 2 * MAX_REL_DIST + 1
NEG_INF = -1e30

D_RNN = D_MODEL
N_RG_BLOCKS = 8
RG_BLOCK = D_RNN // N_RG_BLOCKS
CONV_WIDTH = 4
RG_C = 8.0

D_FF = 4 * D_MODEL
RMS_EPS = 1e-6

kernel_name = "hybrid_chunk_attn_rglru_trunk"


def rmsnorm(x, g):
    xf = x.astype(jnp.float32)
    y = xf * lax.rsqrt(jnp.mean(xf * xf, axis=-1, keepdims=True) + RMS_EPS)
    return (y * g.astype(jnp.float32)).astype(x.dtype)


def chunked_band_attention(h, w_qkv, w_o, rel_table):
    B, S, _ = h.shape
    n_chunks = S // CHUNK
    qkv = jnp.einsum('bsd,de->bse', h, w_qkv).reshape(B, S, 3, N_HEADS, HEAD_DIM)
    q, k, v = qkv[:, :, 0], qkv[:, :, 1], qkv[:, :, 2]
    pad = LEFT_CHUNKS * CHUNK
    k_pad = jnp.pad(k, ((0, 0), (pad, 0), (0, 0), (0, 0)))
    v_pad = jnp.pad(v, ((0, 0), (pad, 0), (0, 0), (0, 0)))
    q_chunks = q.reshape(B, n_chunks, CHUNK, N_HEADS, HEAD_DIM).transpose(1, 0, 2, 3, 4)
    qi = jnp.arange(CHUNK)[:, None]
    kj = jnp.arange(BAND)[None, :]
    rel = qi + pad - kj
    idx = jnp.clip(rel, -MAX_REL_DIST, MAX_REL_DIST) + MAX_REL_DIST
    bias = rel_table[:, idx].astype(jnp.float32)
    scale = HEAD_DIM ** -0.5
    band_offsets = jnp.arange(BAND)

    def one_chunk(args):
        c, q_c = args
        start = c * CHUNK
        k_band = lax.dynamic_slice_in_dim(k_pad, start, BAND, axis=1)
        v_band = lax.dynamic_slice_in_dim(v_pad, start, BAND, axis=1)
        s = jnp.einsum('bqhd,bkhd->bhqk', q_c, k_band).astype(jnp.float32) * scale + bias
        key_pos = start - pad + band_offsets
        s = jnp.where(key_pos[None, None, None, :] >= 0, s, NEG_INF)
        p = jax.nn.softmax(s, axis=-1).astype(v_band.dtype)
        return jnp.einsum('bhqk,bkhd->bqhd', p, v_band)

    o = lax.map(one_chunk, (jnp.arange(n_chunks), q_chunks))
    o = o.transpose(1, 0, 2, 3, 4).reshape(B, S, D_MODEL)
    return jnp.einsum('bsd,de->bse', o, w_o)


def rglru_block(h, w_in, b_in, conv_w, conv_b, w_ga, b_ga, w_gx, b_gx, a_param, w_o, b_o):
    B, S, _ = h.shape
    u = jnp.einsum('bsd,de->bse', h, w_in) + b_in
    x_br = u[..., :D_RNN]
    y_br = jax.nn.gelu(u[..., D_RNN:], approximate=True)
    xp = jnp.pad(x_br, ((0, 0), (CONV_WIDTH - 1, 0), (0, 0)))
    xc = conv_b + xp[:, 0:S] * conv_w[0]
    for tap in range(1, CONV_WIDTH):
        xc = xc + xp[:, tap:tap + S] * conv_w[tap]
    xb = xc.reshape(B, S, N_RG_BLOCKS, RG_BLOCK)
    r = jax.nn.sigmoid(jnp.einsum('bsnc,nce->bsne', xb, w_ga) + b_ga).reshape(B, S, D_RNN)
    ig = jax.nn.sigmoid(jnp.einsum('bsnc,nce->bsne', xb, w_gx) + b_gx).reshape(B, S, D_RNN)
    log_a = -RG_C * r.astype(jnp.float32) * jax.nn.softplus(-a_param.astype(jnp.float32))
    a = jnp.exp(log_a)
    mult = jnp.sqrt(-jnp.expm1(2.0 * log_a))
    b_seq = mult * (ig * xc).astype(jnp.float32)

    def combine(left, right):
        a1, b1 = left
        a2, b2 = right
        return a1 * a2, a2 * b1 + b2

    _, hs = lax.associative_scan(combine, (a, b_seq), axis=1)
    out = hs.astype(h.dtype) * y_br
    return jnp.einsum('bse,ed->bsd', out, w_o) + b_o


def sq_relu_mlp(h, w1, w2):
    z = jax.nn.relu(jnp.einsum('bsd,df->bsf', h, w1))
    return jnp.einsum('bsf,fd->bsd', z * z, w2)


def setup_inputs(seed: int = 0) -> dict:
    key = jax.random.key(seed)
    ks = jax.random.split(key, 20)
    nrm = lambda k, shape, s: jax.random.normal(k, shape, jnp.float32) * s
    x = jax.random.normal(ks[0], (BATCH, SEQ, D_MODEL), jnp.float32)
    norm_mix = 1.0 + nrm(ks[1], (DEPTH, D_MODEL), 0.05)
    norm_mlp = 1.0 + nrm(ks[2], (DEPTH, D_MODEL), 0.05)
    attn_w_qkv = nrm(ks[3], (N_ATTN_LAYERS, D_MODEL, 3 * D_MODEL), D_MODEL ** -0.5)
    attn_w_o = nrm(ks[4], (N_ATTN_LAYERS, D_MODEL, D_MODEL), D_MODEL ** -0.5)
    attn_rel_bias = nrm(ks[5], (N_ATTN_LAYERS, N_HEADS, N_REL), 0.2)
    rec_w_in = nrm(ks[6], (N_REC_LAYERS, D_MODEL, 2 * D_RNN), D_MODEL ** -0.5)
    rec_b_in = nrm(ks[7], (N_REC_LAYERS, 2 * D_RNN), 0.02)
    rec_conv_w = nrm(ks[8], (N_REC_LAYERS, CONV_WIDTH, D_RNN), CONV_WIDTH ** -0.5)
    rec_conv_b = nrm(ks[9], (N_REC_LAYERS, D_RNN), 0.02)
    rec_w_ga = nrm(ks[10], (N_REC_LAYERS, N_RG_BLOCKS, RG_BLOCK, RG_BLOCK), RG_BLOCK ** -0.5)
    rec_b_ga = nrm(ks[11], (N_REC_LAYERS, N_RG_BLOCKS, RG_BLOCK), 0.02)
    rec_w_gx = nrm(ks[12], (N_REC_LAYERS, N_RG_BLOCKS, RG_BLOCK, RG_BLOCK), RG_BLOCK ** -0.5)
    rec_b_gx = nrm(ks[13], (N_REC_LAYERS, N_RG_BLOCKS, RG_BLOCK), 0.02)
    a0 = jax.random.uniform(ks[14], (N_REC_LAYERS, D_RNN), jnp.float32, 0.81, 0.998)
    s0 = a0 ** (1.0 / RG_C)
    rec_a_param = jnp.log(s0) - jnp.log1p(-s0)
    rec_w_o = nrm(ks[15], (N_REC_LAYERS, D_RNN, D_MODEL), D_RNN ** -0.5)
    rec_b_o = nrm(ks[16], (N_REC_LAYERS, D_MODEL), 0.02)
    mlp_w1 = nrm(ks[17], (DEPTH, D_MODEL, D_FF), D_MODEL ** -0.5)
    mlp_w2 = nrm(ks[18], (DEPTH, D_FF, D_MODEL), D_FF ** -0.5)
    norm_final = 1.0 + nrm(ks[19], (D_MODEL,), 0.05)
    return {"x": x, "norm_mix": norm_mix, "norm_mlp": norm_mlp,
            "attn_w_qkv": attn_w_qkv, "attn_w_o": attn_w_o, "attn_rel_bias": attn_rel_bias,
            "rec_w_in": rec_w_in, "rec_b_in": rec_b_in, "rec_conv_w": rec_conv_w,
            "rec_conv_b": rec_conv_b, "rec_w_ga": rec_w_ga, "rec_b_ga": rec_b_ga,
            "rec_w_gx": rec_w_gx, "rec_b_gx": rec_b_gx, "rec_a_param": rec_a_param,
            "rec_w_o": rec_w_o, "rec_b_o": rec_b_o,
            "mlp_w1": mlp_w1, "mlp_w2": mlp_w2, "norm_final": norm_final}


def reference(x, norm_mix, norm_mlp, attn_w_qkv, attn_w_o, attn_rel_bias,
              rec_w_in, rec_b_in, rec_conv_w, rec_conv_b, rec_w_ga, rec_b_ga,
              rec_w_gx, rec_b_gx, rec_a_param, rec_w_o, rec_b_o,
              mlp_w1, mlp_w2, norm_final):
    h = x
    for layer in range(DEPTH):
        hn = rmsnorm(h, norm_mix[layer])
        j = layer // N_MIXERS
        if layer % N_MIXERS == 0:
            mix = chunked_band_attention(hn, attn_w_qkv[j], attn_w_o[j], attn_rel_bias[j])
        else:
            mix = rglru_block(hn, rec_w_in[j], rec_b_in[j], rec_conv_w[j], rec_conv_b[j],
                              rec_w_ga[j], rec_b_ga[j], rec_w_gx[j], rec_b_gx[j],
                              rec_a_param[j], rec_w_o[j], rec_b_o[j])
        h = h + mix
        h = h + sq_relu_mlp(rmsnorm(h, norm_mlp[layer]), mlp_w1[layer], mlp_w2[layer])
    return rmsnorm(h, norm_final)
```

```python
from contextlib import ExitStack
import numpy as np
import concourse.bass as bass
import concourse.mybir as mybir
from concourse.bass_utils import run_bass_kernel_spmd

F32 = mybir.dt.float32
BF16 = mybir.dt.bfloat16
AF = mybir.ActivationFunctionType
ALU = mybir.AluOpType

D = 1024
SEQ = 2048
NTG = 4
TG = 512
NDC = 8
NSLOT = 3
SLOT_N = 4096
NEG = -30000.0
EPS = 1e-6
DEBUG = {}


class _Op:
    __slots__ = ("idx", "eng", "fn", "deps", "chan", "sig", "waits", "has_dep")


class Sched:
    def __init__(self, nc, same_engine_sync=True):
        self.nc = nc
        self.ops = []
        self.last_writer = {}
        self.readers = {}
        self.same_engine_sync = same_engine_sync

    def add(self, eng, fn, reads=(), writes=(), ex=(), chan=None):
        op = _Op()
        op.idx = len(self.ops)
        op.eng = eng
        op.fn = fn
        op.chan = chan
        op.sig = None
        op.has_dep = False
        reads = list(reads)
        writes = list(writes) + list(ex)
        deps = set()
        for k in reads:
            w = self.last_writer.get(k)
            if w is not None:
                deps.add(w)
        for k in writes:
            w = self.last_writer.get(k)
            if w is not None:
                deps.add(w)
            rd = self.readers.get(k)
            if rd:
                for v in rd.values():
                    if isinstance(v, list):
                        deps.update(v)
                    else:
                        deps.add(v)
        for k in reads:
            rd = self.readers.setdefault(k, {})
            if chan is not None:
                rd.setdefault(("dma", eng), []).append(op.idx)
            else:
                rd[eng] = op.idx
        for k in writes:
            self.last_writer[k] = op.idx
            self.readers[k] = {}
        deps.discard(op.idx)
        op.deps = deps
        self.ops.append(op)
        return op

    def finalize(self, stack):
        nc = self.nc
        ops = self.ops
        for op in ops:
            keep = set()
            for d in op.deps:
                dop = ops[d]
                if dop.chan is None and op.chan is None and dop.eng == op.eng:
                    if dop.eng == "pe" or not self.same_engine_sync:
                        continue
                keep.add(d)
            op.deps = keep
            for d in keep:
                ops[d].has_dep = True
        sems = {}
        counts = {}

        def get_sem(name):
            if name not in sems:
                sems[name] = stack.enter_context(nc.semaphore("s_" + name))
                counts[name] = 0

        for op in ops:
            if op.chan is not None:
                nm = "c_" + op.chan
                get_sem(nm)
                counts[nm] += 16
                op.sig = (nm, counts[nm])
            elif op.has_dep:
                nm = "e_" + op.eng
                get_sem(nm)
                counts[nm] += 1
                op.sig = (nm, counts[nm])
        waited = {}
        for op in ops:
            need = {}
            for d in op.deps:
                s, v = ops[d].sig
                if v > need.get(s, 0):
                    need[s] = v
            w = waited.setdefault(op.eng, {})
            op.waits = []
            for s, v in need.items():
                if w.get(s, 0) < v:
                    w[s] = v
                    op.waits.append((s, v))
        self.sems = sems
        self.counts = counts
        per_eng = {}
        for op in ops:
            per_eng.setdefault(op.eng, []).append(op)
        self.per_eng = per_eng

    def emit(self, block):
        sems = self.sems
        per_eng = self.per_eng
        counts = self.counts

        def run(engname, e):
            for op in per_eng.get(engname, []):
                for s, v in op.waits:
                    e.wait_ge(sems[s], v)
                ins = op.fn(e)
                if op.sig is not None:
                    ins.then_inc(sems[op.sig[0]], 16 if op.chan is not None else 1)
            if engname == "sp":
                for s, v in counts.items():
                    if v > 0:
                        e.wait_ge(sems[s], v)

        @block.sync
        def _(e):
            run("sp", e)

        @block.tensor
        def _(e):
            run("pe", e)

        @block.scalar
        def _(e):
            run("act", e)

        @block.vector
        def _(e):
            run("dve", e)

        @block.gpsimd
        def _(e):
            run("pool", e)


def _piece(w_sub):
    k, cols = w_sub.shape
    nch = k // 128
    out = np.ascontiguousarray(w_sub.reshape(nch, 128, cols).transpose(1, 0, 2)).reshape(128, nch * cols)
    if out.shape[1] < SLOT_N:
        pad = np.zeros((128, SLOT_N - out.shape[1]), np.float32)
        out = np.concatenate([out, pad], axis=1)
    return out


def _col(v):
    return np.ascontiguousarray(v.reshape(-1, 128).T)


def layer_kinds(layers):
    return ["attn" if l % 2 == 0 else "rec" for l in layers]


def prep_weights(inp, layers, final):
    pieces = []
    sm = []
    sm_off = {}

    def add_sm(name, arr):
        sm_off[name] = sum(a.shape[1] for a in sm)
        sm.append(np.ascontiguousarray(arr, dtype=np.float32))

    relb = []
    gates = []
    maskc = np.zeros((128, 256), np.float32)
    maskc[64:128, 0:64] = NEG
    add_sm("maskc", maskc)
    kq = np.arange(128)
    rel0 = kq[None, :] - kq[:, None]
    idx_d0 = np.clip(rel0, -128, 128) + 128
    idx_d1 = np.clip(rel0 + 128, -128, 128) + 128
    for l in layers:
        j = l // 2
        add_sm(("gmix", l), _col(inp["norm_mix"][l]))
        add_sm(("gmlp", l), _col(inp["norm_mlp"][l]))
        if l % 2 == 0:
            wqkv = inp["attn_w_qkv"][j]
            for p in range(8):
                sub = np.concatenate([wqkv[:, p * 128:(p + 1) * 128],
                                      wqkv[:, 1024 + p * 128:1024 + (p + 1) * 128],
                                      wqkv[:, 2048 + p * 128:2048 + (p + 1) * 128]], axis=1)
                pieces.append(_piece(sub))
                tab = inp["attn_rel_bias"][j]
                rb = np.zeros((128, 512), np.float32)
                for hh in range(2):
                    t = tab[2 * p + hh]
                    rb[:, hh * 256:hh * 256 + 128] = t[idx_d0]
                    rb[:, hh * 256 + 128:hh * 256 + 256] = t[idx_d1]
                relb.append(rb)
            wo = inp["attn_w_o"][j]
            for hf in range(2):
                pieces.append(_piece(wo[:, hf * 512:(hf + 1) * 512]))
            add_sm(("chcol", l), np.broadcast_to(inp["attn_rel_bias"][j][:, 256].reshape(1, 16), (128, 16)))
        else:
            ga = inp["rec_w_ga"][j]
            gx = inp["rec_w_gx"][j]
            gp = np.concatenate([ga.transpose(1, 0, 2).reshape(128, 1024), gx.transpose(1, 0, 2).reshape(128, 1024)], axis=1)
            gates.append(np.ascontiguousarray(gp))
            win = inp["rec_w_in"][j]
            for c in range(8):
                sub = np.concatenate([win[:, c * 128:(c + 1) * 128], win[:, 1024 + c * 128:1024 + (c + 1) * 128]], axis=1)
                pieces.append(_piece(sub))
            wo = inp["rec_w_o"][j]
            for hf in range(2):
                pieces.append(_piece(wo[:, hf * 512:(hf + 1) * 512]))
            add_sm(("b_in", l), _col(inp["rec_b_in"][j]))
            add_sm(("conv_w", l), np.concatenate([_col(inp["rec_conv_w"][j][t]) for t in range(4)], axis=1))
            add_sm(("conv_b", l), _col(inp["rec_conv_b"][j]))
            add_sm(("b_ga", l), _col(inp["rec_b_ga"][j].reshape(-1)))
            add_sm(("b_gx", l), _col(inp["rec_b_gx"][j].reshape(-1)))
            add_sm(("a_param", l), _col(inp["rec_a_param"][j]))
            add_sm(("b_o", l), _col(inp["rec_b_o"][j]))
        w1 = inp["mlp_w1"][l]
        w2 = inp["mlp_w2"][l]
        for g in range(8):
            pieces.append(_piece(w1[:, g * 512:(g + 1) * 512]))
            pieces.append(_piece(w2[g * 512:(g + 1) * 512, :]))
    if final:
        add_sm("gfin", _col(inp["norm_final"]))
    wbig = np.ascontiguousarray(np.stack(pieces, axis=0))
    wsm = np.ascontiguousarray(np.concatenate(sm, axis=1))
    if not relb:
        relb = [np.zeros((128, 512), np.float32)]
    relb = np.ascontiguousarray(np.stack(relb, axis=0))
    if not gates:
        gates = [np.zeros((128, 2048), np.float32)]
    wgate = np.ascontiguousarray(np.stack(gates, axis=0))
    return wbig, wsm, relb, wgate, sm_off


def build_program(layers, final, n_pieces, n_sm, n_relb, n_gate, sm_off):
    nc = bass.Bass("TRN2", target_bir_lowering=False)
    xT = nc.dram_tensor("xT", [D, SEQ], F32, kind="ExternalInput").ap()
    wbig = nc.dram_tensor("wbig", [n_pieces, 128, SLOT_N], F32, kind="ExternalInput").ap()
    wsm_d = nc.dram_tensor("wsm", [128, n_sm], F32, kind="ExternalInput").ap()
    relb_d = nc.dram_tensor("relb", [n_relb, 128, 512], F32, kind="ExternalInput").ap()
    wgate_d = nc.dram_tensor("wgate", [n_gate, 128, 2048], F32, kind="ExternalInput").ap()
    outT = nc.dram_tensor("outT", [D, SEQ], F32, kind="ExternalOutput").ap()
    xT_v = xT.rearrange("(c p) t -> p c t", p=128)
    outT_v = outT.rearrange("(c p) t -> p c t", p=128)

    with ExitStack() as st:
        def sb(name, shape, dt):
            return st.enter_context(nc.sbuf_tensor("k_" + name, shape, dt))

        h = sb("h", [128, NDC, SEQ], F32)
        hn = sb("hn", [128, NDC, SEQ], BF16)
        wsl = sb("wsl", [128, NSLOT, SLOT_N], BF16)
        wsm = sb("wsm_sb", [128, n_sm], F32)
        ones = sb("ones", [128, 128], BF16)
        dummy = sb("dummy", [128, 8], F32)
        rder = sb("rder", [128, 40], F32)
        gatebuf = sb("gatebuf", [128, 2048], BF16)
        ARENA_W = 20480
        arena = sb("arena", [128, ARENA_W], F32)
        ps = st.enter_context(nc.psum_tensor("k_ps", [128, 8, 512], F32))
        arena_t = {}

        def carve(spec):
            arena_t.clear()
            off = 0
            for ent in spec:
                name, shape, dt = ent[0], ent[1], ent[2]
                if len(ent) > 3:
                    off = ent[3]
                n = int(np.prod(shape))
                words = n // 2 if dt == BF16 else n
                v = arena[:, off:off + words]
                if dt == BF16:
                    v = v.bitcast(BF16)
                if len(shape) == 2:
                    v = v.rearrange("p (a b) -> p a b", b=shape[1])
                elif len(shape) == 3:
                    v = v.rearrange("p (a b c) -> p a b c", b=shape[1], c=shape[2])
                arena_t[name] = v
                off += words
            assert off <= ARENA_W, (off, ARENA_W)

        S = Sched(nc)
        AE = ["AE_A", "AE_B", "AE_C"]

        def arena_join(keys):
            S.add("pool", lambda e: e.memset(dummy[:, 0:1], 0.0), writes=list(keys) + ["dummy"])

        state = {"next_load": 0, "next_use": 0}

        def wload():
            k = state["next_load"]
            if k >= n_pieces:
                return
            state["next_load"] += 1
            slot = k % NSLOT
            src = wbig[k].rearrange("p (a b) -> p a b", b=1024)
            dst = wsl[:, slot, :].rearrange("p (a b) -> p a b", b=1024)
            S.add("pool", lambda e: e.dma_start(out=dst, in_=src), writes=[("w", slot)], chan="w%d" % slot)

        def wnext():
            k = state["next_use"]
            state["next_use"] += 1
            assert k < state["next_load"], "piece used before load emitted"
            return k % NSLOT

        def wdone():
            wload()

        def smc(name, i=0, n=1):
            o = sm_off[name] + i
            return wsm[:, o:o + n]

        S.add("sp", lambda e: e.dma_start(out=wsm[:], in_=wsm_d[:, :]), writes=["wsm"], chan="wsm")
        for tg in range(NTG):
            for hf in range(2):
                S.add("sp", lambda e, tg=tg, hf=hf: e.dma_start(out=h[:, hf * 4:(hf + 1) * 4, tg * TG:(tg + 1) * TG], in_=xT_v[:, hf * 4:(hf + 1) * 4, tg * TG:(tg + 1) * TG]),
                      writes=[("h", dc, tg) for dc in range(hf * 4, (hf + 1) * 4)], chan="hin%d" % (tg * 2 + hf))
        S.add("dve", lambda e: e.memset(ones[:], 1.0), writes=["ones"])
        for _ in range(NSLOT):
            wload()

        def rmsnorm(gname, to_h=False):
            AE[:] = ["AE_B"]
            arena_join(["AE_B"])
            sq = arena_t["sq"]
            rstd = [arena_t["rstd0"], arena_t["rstd1"]]
            stmp = [arena_t["stmp0"], arena_t["stmp1"]]
            for tg in range(NTG):
                par = tg % 2
                tsl = slice(tg * TG, (tg + 1) * TG)
                bank = par
                for dc in range(NDC):
                    S.add("act", lambda e, dc=dc, tsl=tsl: e.activation(out=sq[:, dc, :], in_=h[:, dc, tsl], func=AF.Square),
                          reads=[("h", dc, tg), *AE], writes=[("sq", dc)])
                for dc in range(NDC):
                    S.add("pe", lambda e, dc=dc, bank=bank: e.matmul(ps[:, bank, :], lhsT=ones[:], rhs=sq[:, dc, :], start=(dc == 0), stop=(dc == NDC - 1)),
                          reads=[("sq", dc), "ones", *AE], ex=[("ps", bank)])
                if to_h:
                    S.add("act", lambda e, par=par, bank=bank: e.activation(out=stmp[par][:], in_=ps[:, bank, :], func=AF.Sqrt, scale=1.0 / D, bias=EPS),
                          reads=[*AE], writes=[("stmp", par)], ex=[("ps", bank)])
                    S.add("dve", lambda e, par=par: e.reciprocal(out=rstd[par][:], in_=stmp[par][:]),
                          reads=[("stmp", par), *AE], writes=[("rstd", par)])
                else:
                    S.add("act", lambda e, par=par, bank=bank: e.activation(out=stmp[par][:], in_=ps[:, bank, :], func=AF.Ln, scale=1.0 / D, bias=EPS),
                          reads=[*AE], writes=[("stmp", par)], ex=[("ps", bank)])
                    S.add("act", lambda e, par=par: e.activation(out=rstd[par][:], in_=stmp[par][:], func=AF.Exp, scale=-0.5),
                          reads=[("stmp", par), *AE], writes=[("rstd", par)])
                for dc in range(NDC):
                    if to_h:
                        S.add("dve", lambda e, dc=dc, par=par, tsl=tsl: e.scalar_tensor_tensor(out=h[:, dc, tsl], in0=h[:, dc, tsl], scalar=smc(gname, dc), in1=rstd[par][:], op0=ALU.mult, op1=ALU.mult),
                              reads=[("h", dc, tg), ("rstd", par), "wsm", *AE], writes=[("h", dc, tg)])
                    else:
                        S.add("dve", lambda e, dc=dc, par=par, tsl=tsl: e.scalar_tensor_tensor(out=hn[:, dc, tsl], in0=h[:, dc, tsl], scalar=smc(gname, dc), in1=rstd[par][:], op0=ALU.mult, op1=ALU.mult),
                              reads=[("h", dc, tg), ("rstd", par), "wsm", *AE], writes=[("hn", dc, tg)])

        def mlp():
            AE[:] = ["AE_A"]
            arena_join(["AE_A"])
            z = arena_t["z"]
            rtmp = arena_t["rtmp"]
            cnt = {"a": 0, "b": 0}
            for g in range(8):
                s1 = wnext()
                w1v = wsl[:, s1, :].rearrange("p (c f) -> p c f", f=512)
                for tg in range(NTG):
                    tsl = slice(tg * TG, (tg + 1) * TG)
                    for fc in range(4):
                        bank = cnt["a"] % 4
                        rt = cnt["a"] % 4
                        cnt["a"] += 1
                        for dc in range(NDC):
                            S.add("pe", lambda e, dc=dc, fc=fc, bank=bank, tsl=tsl, w1v=w1v: e.matmul(ps[:, bank, :], lhsT=w1v[:, dc, fc * 128:(fc + 1) * 128], rhs=hn[:, dc, tsl], start=(dc == 0), stop=(dc == NDC - 1)),
                                  reads=[("w", s1), ("hn", dc, tg)], ex=[("ps", bank)])
                        S.add("act", lambda e, bank=bank, rt=rt: e.activation(out=rtmp[:, rt, :], in_=ps[:, bank, :], func=AF.Relu),
                              reads=[*AE], writes=[("rtmp", rt)], ex=[("ps", bank)])
                        S.add("act", lambda e, rt=rt, fc=fc, tsl=tsl: e.activation(out=z[:, fc, tsl], in_=rtmp[:, rt, :], func=AF.Square),
                              reads=[("rtmp", rt), *AE], writes=[("z", fc, tg)])
                wdone()
                s2 = wnext()
                w2v = wsl[:, s2, :].rearrange("p (c f) -> p c f", f=1024)
                for tg in range(NTG):
                    tsl = slice(tg * TG, (tg + 1) * TG)
                    for dcs in range(NDC):
                        bank = 4 + cnt["b"] % 4
                        cnt["b"] += 1
                        for fc in range(4):
                            S.add("pe", lambda e, fc=fc, dcs=dcs, bank=bank, tsl=tsl, w2v=w2v: e.matmul(ps[:, bank, :], lhsT=w2v[:, fc, dcs * 128:(dcs + 1) * 128], rhs=z[:, fc, tsl], start=(fc == 0), stop=(fc == 3)),
                                  reads=[("w", s2), ("z", fc, tg), *AE], ex=[("ps", bank)])
                        S.add("dve", lambda e, dcs=dcs, bank=bank, tsl=tsl: e.tensor_tensor(out=h[:, dcs, tsl], in0=ps[:, bank, :], in1=h[:, dcs, tsl], op=ALU.add),
                              reads=[("h", dcs, tg)], writes=[("h", dcs, tg)], ex=[("ps", bank)])
                wdone()

        def out_proj(oT, bias_name=None):
            AE[:] = ["AE_C"]
            cnt = 0
            ss = [wnext(), wnext()]
            wvs = [wsl[:, sx, :].rearrange("p (c f) -> p c f", f=512) for sx in ss]
            for tg in range(NTG):
                tsl = slice(tg * TG, (tg + 1) * TG)
                for hf in range(2):
                    s_, wv = ss[hf], wvs[hf]
                    for dcs in range(4):
                        dco = hf * 4 + dcs
                        bank = 6 + cnt % 2
                        cnt += 1
                        for pc in range(NDC):
                            S.add("pe", lambda e, pc=pc, dcs=dcs, bank=bank, tsl=tsl, wv=wv: e.matmul(ps[:, bank, :], lhsT=wv[:, pc, dcs * 128:(dcs + 1) * 128], rhs=oT[:, pc, tsl], start=(pc == 0), stop=(pc == NDC - 1)),
                                  reads=[("w", s_), ("oT", pc, tg), *AE], ex=[("ps", bank)])
                        if bias_name is None:
                            S.add("dve", lambda e, dco=dco, bank=bank, tsl=tsl: e.tensor_tensor(out=h[:, dco, tsl], in0=ps[:, bank, :], in1=h[:, dco, tsl], op=ALU.add),
                                  reads=[("h", dco, tg)], writes=[("h", dco, tg)], ex=[("ps", bank)])
                        else:
                            S.add("dve", lambda e, dco=dco, bank=bank, tsl=tsl: e.scalar_tensor_tensor(out=h[:, dco, tsl], in0=ps[:, bank, :], scalar=smc(bias_name, dco), in1=h[:, dco, tsl], op0=ALU.add, op1=ALU.add),
                                  reads=[("h", dco, tg), "wsm"], writes=[("h", dco, tg)], ex=[("ps", bank)])
            wdone()
            wdone()

        relb_state = {"n": 0}

        def attention(l):
            AE[:] = ["AE_A", "AE_B"]
            arena_join(["AE_A", "AE_B", "AE_C"])
            oT = arena_t["oT"]
            QT = [arena_t["QT0"], arena_t["QT1"]]
            KAB = [[arena_t["KA0"], arena_t["KB0"]], [arena_t["KA1"], arena_t["KB1"]]]
            expB = [arena_t["expB0"], arena_t["expB1"]]
            Vx = [arena_t["Vx0"], arena_t["Vx1"]]
            expS = [arena_t["expS%d" % i] for i in range(4)]
            rc = [arena_t["rc0"], arena_t["rc1"]]
            rbuf = [arena_t["relb0"], arena_t["relb0"]]
            Bm = arena_t["relb0"]
            for b in range(2):
                S.add("pool", lambda e, b=b: e.memset(Vx[b][:, :, 64:128], 1.0), reads=[*AE], writes=[("Vx_ones", b)])
                S.add("pool", lambda e, b=b: e.memset(KAB[b][1][0:64, :], 0.0), reads=[*AE], writes=[("Kz", b)])
            cnts = {"ss0": 0, "ss1": 0, "po": 0, "pj": 0}
            PJ_BANKS = [7]

            def load_relb(p):
                gi = relb_state["n"]
                relb_state["n"] += 1
                S.add("sp", lambda e, gi=gi, p=p: e.dma_start(out=rbuf[p % 2][:].rearrange("p a b -> p (a b)"), in_=relb_d[gi]), reads=[*AE], writes=[("relb", 0)], chan="relb0")

            def proj_units(p, banks):
                pb = p % 2
                s = wnext()
                wv = wsl[:, s, 0:8 * 384].rearrange("p (c f) -> p c f", f=384)
                units = []
                for which in (0, 1):
                    for tg in range(NTG):
                        ubank = {}

                        def ua(which=which, tg=tg, ubank=ubank):
                            tsl = slice(tg * TG, (tg + 1) * TG)
                            bank = banks[cnts["pj"] % len(banks)]
                            cnts["pj"] += 1
                            ubank[0] = bank
                            for dc in range(NDC // 2):
                                S.add("pe", lambda e, dc=dc: e.matmul(ps[:, bank, :], lhsT=wv[:, dc, which * 128:(which + 1) * 128], rhs=hn[:, dc, tsl], start=(dc == 0), stop=False),
                                      reads=[("w", s), ("hn", dc, tg)], ex=[("ps", bank)])
                        units.append(ua)

                        def u(which=which, tg=tg, ubank=ubank):
                            tsl = slice(tg * TG, (tg + 1) * TG)
                            bank = ubank[0]
                            for dc in range(NDC // 2, NDC):
                                S.add("pe", lambda e, dc=dc: e.matmul(ps[:, bank, :], lhsT=wv[:, dc, which * 128:(which + 1) * 128], rhs=hn[:, dc, tsl], start=False, stop=(dc == NDC - 1)),
                                      reads=[("w", s), ("hn", dc, tg)], ex=[("ps", bank)])
                            if which == 0:
                                S.add("dve", lambda e: e.tensor_scalar_mul(out=QT[pb][:, tsl], in0=ps[:, bank, :], scalar1=0.125),
                                      reads=[*AE], writes=[("qk", pb, 0, tg)], ex=[("ps", bank)])
                            else:
                                S.add("dve", lambda e: e.tensor_copy(out=KAB[pb][0][:, tsl], in_=ps[:, bank, :]),
                                      reads=[*AE], writes=[("qk", pb, 1, tg)], ex=[("ps", bank)])
                                S.add("pool", lambda e: e.tensor_copy(out=KAB[pb][1][64:128, tsl], in_=KAB[pb][0][64:128, tsl]),
                                      reads=[("qk", pb, 1, tg), *AE], writes=[("qk", pb, 2, tg)])
                                S.add("pool", lambda e: e.memset(KAB[pb][0][64:128, tsl], 0.0),
                                      reads=[*AE], writes=[("qk", pb, 1, tg)])
                        units.append(u)
                vbank = {}
                for t in range(16):
                    def u(t=t):
                        tq, t4 = t // 4, t % 4
                        if t4 == 0:
                            vbank[tq] = banks[cnts["pj"] % len(banks)]
                            cnts["pj"] += 1
                        bank = vbank[tq]
                        for dc in range(NDC):
                            S.add("pe", lambda e, dc=dc: e.matmul(ps[:, bank, t4 * 128:(t4 + 1) * 128], lhsT=hn[:, dc, t * 128:(t + 1) * 128], rhs=wv[:, dc, 256:384], start=(dc == 0), stop=(dc == NDC - 1)),
                                  reads=[("w", s), ("hn", dc, t // 4)], ex=[("ps", bank)])
                        if t4 == 3:
                            psv = ps[:, bank, :].rearrange("p (t c) -> p t c", c=128)
                            S.add("dve", lambda e: e.tensor_copy(out=Vx[pb][:, tq * 4:(tq + 1) * 4, 0:64], in_=psv[:, :, 0:64]),
                                  reads=[*AE], writes=[("V", pb, tq, 0)], ex=[("ps", bank)])
                            S.add("dve", lambda e: e.tensor_copy(out=Vx[pb][:, tq * 4:(tq + 1) * 4, 128:192], in_=psv[:, :, 64:128]),
                                  reads=[*AE], writes=[("V", pb, tq, 1)], ex=[("ps", bank)])
                    units.append(u)
                units.append(wdone)
                return units

            load_relb(0)
            for u in proj_units(0, [0, 1, 2, 3, 7]):
                u()
            for p in range(8):
                pb = p % 2
                if p + 1 < 8:
                    units = proj_units(p + 1, PJ_BANKS)
                else:
                    units = []
                for hh in range(2):
                    S.add("dve", lambda e, hh=hh, p=p: e.scalar_tensor_tensor(out=Bm[:, hh, :], in0=Bm[:, hh, :], scalar=smc(("chcol", l), 2 * p + hh), in1=smc("maskc", 0, 256), op0=ALU.subtract, op1=ALU.add),
                          reads=["wsm", *AE], writes=[("relb", 0)])
                    S.add("act", lambda e, hh=hh, p=p: e.activation(out=expB[p % 2][:, hh, :], in_=Bm[:, hh, :], func=AF.Exp),
                          reads=[("relb", 0), *AE], writes=[("expB", p % 2, hh)])
                if p + 1 < 8:
                    load_relb(p + 1)
                items = []
                for G in range(4):
                    for t in range(max(0, 4 * G - 4), 4 * G + 4):
                        for hh in range(2):
                            items.append((G, t, hh))
                nitems = len(items)
                ntot = max(len(units), 1)
                sset_of = {}

                def geom(G, t):
                    j0 = max(t, 4 * G)
                    j1 = min(t + 4, 4 * G + 3)
                    return j0, j1, (j1 - j0 + 1) * 128

                def scores(item, p=p, pb=pb):
                    G, t, hh = item
                    r0 = hh * 64
                    j0, j1, N = geom(G, t)
                    k = cnts["ss0"]
                    cnts["ss0"] += 1
                    es = expS[k % 4]
                    ek = k % 4
                    sset_of[item] = (es, ek)
                    bank = k % 3
                    S.add("pe", lambda e: e.matmul(ps[:, bank, 0:N], lhsT=KAB[pb][hh][:, t * 128:(t + 1) * 128], rhs=QT[pb][:, j0 * 128:(j1 + 1) * 128], start=True, stop=True),
                          reads=[("qk", pb, 0, G), ("qk", pb, 1, t // 4), ("qk", pb, 2, t // 4), ("Kz", pb), *AE], ex=[("ps", bank)])
                    S.add("act", lambda e: e.activation(out=es[:, 0:N], in_=ps[:, bank, 0:N], func=AF.Exp),
                          reads=[*AE], writes=[("expS", ek, 0), ("expS", ek, 1)], ex=[("ps", bank)])
                    if j1 == t + 4:
                        S.add("pool", lambda e: e.memset(es[0:64, N - 64:N], 0.0), reads=[*AE], writes=[("expS", ek, 1)])
                    d0 = j0 - t
                    if d0 == 0:
                        nn = min(2, (j1 - j0 + 1))
                        S.add("dve", lambda e: e.tensor_tensor(out=es[:, 0:nn * 128], in0=es[:, 0:nn * 128], in1=expB[p % 2][:, hh, 0:nn * 128], op=ALU.mult),
                              reads=[("expB", p % 2, hh), *AE], writes=[("expS", ek, 0)])
                    elif d0 == 1:
                        S.add("dve", lambda e: e.tensor_tensor(out=es[:, 0:128], in0=es[:, 0:128], in1=expB[p % 2][:, hh, 128:256], op=ALU.mult),
                              reads=[("expB", p % 2, hh), *AE], writes=[("expS", ek, 0)])

                def pv(item, p=p, pb=pb):
                    G, t, hh = item
                    j0, j1, N = geom(G, t)
                    es, ek = sset_of[item]
                    vcols = slice(0, 128) if hh == 0 else slice(64, 192)
                    so = slice(64, 128) if hh == 0 else slice(0, 64)
                    oo = slice(0, 64) if hh == 0 else slice(64, 128)
                    bank = 3 + 2 * hh + G % 2
                    c0 = (j0 - 4 * G) * 128
                    first = (t == max(0, 4 * G - 4))
                    last = (t == 4 * G + 3)
                    S.add("pe", lambda e: e.matmul(ps[:, bank, c0:c0 + N], lhsT=Vx[pb][:, t, vcols], rhs=es[:, 0:N], start=first, stop=last, skip_group_check=True),
                          reads=[("expS", ek, 0), ("expS", ek, 1), ("V", pb, t // 4, 0), ("V", pb, t // 4, 1), ("Vx_ones", pb), *AE], ex=[("ps", bank)])
                    if last:
                        par = cnts["po"] % 2
                        cnts["po"] += 1
                        tsl = slice(G * TG, (G + 1) * TG)
                        S.add("act", lambda e: e.activation(out=rc[par][so, :], in_=ps[so, bank, :], func=AF.Ln),
                              reads=[*AE], writes=[("rc", par)], ex=[("ps", bank)])
                        S.add("act", lambda e: e.activation(out=rc[par][so, :], in_=rc[par][so, :], func=AF.Exp, scale=-1.0),
                              reads=[("rc", par), *AE], writes=[("rc", par)])
                        S.add("dve", lambda e: e.tensor_tensor(out=oT[oo, p, tsl], in0=ps[oo, bank, :], in1=rc[par][so, :], op=ALU.mult),
                              reads=[("rc", par), "AE_C", *AE], writes=[("oT", p, G)], ex=[("ps", bank)])

                LA = 3
                for i in range(nitems + LA):
                    if i < nitems:
                        scores(items[i])
                    if i >= LA:
                        pv(items[i - LA])
                        step = i - LA + 1
                        while units and len(units) * nitems > (nitems - step) * ntot:
                            units.pop(0)()
                while units:
                    units.pop(0)()
            out_proj(oT, None)

        gate_state = {"n": 0}

        def recurrent(l):
            AE[:] = ["AE_A", "AE_B"]
            arena_join(["AE_A", "AE_B", "AE_C"])
            oT = arena_t["oT"]
            yb = [arena_t["yb0"], arena_t["yb1"]]
            xh = [arena_t["xh0"], arena_t["xh1"]]
            xc = [arena_t["xc0"], arena_t["xc1"], arena_t["xc2"], arena_t["xc3"]]
            xcb = [arena_t["xcb0"], arena_t["xcb1"]]
            ab = [arena_t["ab0"], arena_t["ab1"]]
            mb = [arena_t["mb0"], arena_t["mb1"]]
            tb = [arena_t["tb0"], arena_t["tb1"], arena_t["tb2"]]
            S.add("act", lambda e: e.activation(out=rder[:, 32:40], in_=smc(("a_param", l), 0, 8), func=AF.Exp, scale=-1.0),
                  reads=["wsm"], writes=["rder_t"])
            S.add("act", lambda e: e.activation(out=rder[:, 32:40], in_=rder[:, 32:40], func=AF.Ln, bias=1.0),
                  reads=["rder_t"], writes=["rder_t"])
            S.add("dve", lambda e: e.tensor_scalar_mul(out=rder[:, 0:8], in0=rder[:, 32:40], scalar1=-8.0), reads=["rder_t"], writes=["rder"])
            S.add("dve", lambda e: e.tensor_scalar_mul(out=rder[:, 8:16], in0=rder[:, 32:40], scalar1=-4.0), reads=["rder_t"], writes=["rder"])
            S.add("dve", lambda e: e.tensor_scalar_mul(out=rder[:, 16:24], in0=smc(("b_ga", l), 0, 8), scalar1=0.5), reads=["wsm"], writes=["rder"])
            S.add("dve", lambda e: e.tensor_scalar_mul(out=rder[:, 24:32], in0=smc(("b_gx", l), 0, 8), scalar1=0.5), reads=["wsm"], writes=["rder"])
            gi = gate_state["n"]
            gate_state["n"] += 1
            S.add("pool", lambda e, gi=gi: e.dma_start(out=gatebuf[:].rearrange("p (a b) -> p a b", b=1024), in_=wgate_d[gi].rearrange("p (a b) -> p a b", b=1024)),
                  writes=["gate"], chan="gate")
            wg = gatebuf[:].rearrange("p (g n e) -> p g n e", g=2, n=8)
            cnt = {"pj": 0}
            wstate = {}

            def wv_of(c):
                if c not in wstate:
                    s = wnext()
                    wstate[c] = (s, wsl[:, s, 0:2048].rearrange("p (c f) -> p c f", f=256))
                return wstate[c]

            def stageY(c):
                s, wv = wv_of(c)
                for tg in range(NTG):
                    tsl = slice(tg * TG, (tg + 1) * TG)
                    bank = cnt["pj"] % 4
                    cnt["pj"] += 1
                    for dc in range(NDC):
                        S.add("pe", lambda e, dc=dc, bank=bank, tsl=tsl: e.matmul(ps[:, bank, :], lhsT=wv[:, dc, 128:256], rhs=hn[:, dc, tsl], start=(dc == 0), stop=(dc == NDC - 1)),
                              reads=[("w", s), ("hn", dc, tg)], ex=[("ps", bank)])
                    S.add("act", lambda e, bank=bank, tsl=tsl: e.activation(out=yb[c % 2][:, tsl], in_=ps[:, bank, :], func=AF.Gelu_apprx_tanh, bias=smc(("b_in", l), 8 + c)),
                          reads=["wsm", *AE], writes=[("yb", c % 2, tg)], ex=[("ps", bank)])

            def stageA(n):
                c, tg = n // 4, n % 4
                s, wv = wv_of(c)
                p2, p3 = n % 2, n % 4
                tsl = slice(tg * TG, (tg + 1) * TG)
                bank = cnt["pj"] % 4
                cnt["pj"] += 1
                for dc in range(NDC):
                    S.add("pe", lambda e, dc=dc: e.matmul(ps[:, bank, :], lhsT=wv[:, dc, 0:128], rhs=hn[:, dc, tsl], start=(dc == 0), stop=(dc == NDC - 1)),
                          reads=[("w", s), ("hn", dc, tg)], ex=[("ps", bank)])
                if tg == 3:
                    wdone()
                S.add("dve", lambda e: e.tensor_scalar_add(out=xh[p2][:, 3:515], in0=ps[:, bank, :], scalar1=smc(("b_in", l), c)),
                      reads=["wsm", *AE], writes=[("xh", p2)], ex=[("ps", bank)])
                if tg == 0:
                    S.add("pool", lambda e: e.memset(xh[p2][:, 0:3], 0.0), reads=[*AE], writes=[("xhalo", p2)])
                else:
                    S.add("pool", lambda e: e.tensor_copy(out=xh[p2][:, 0:3], in_=xh[1 - p2][:, 512:515]),
                          reads=[("xh", 1 - p2), *AE], writes=[("xhalo", p2)])
                S.add("pool", lambda e: e.tensor_scalar(out=xc[p3][:], in0=xh[p2][:, 0:512], scalar1=smc(("conv_w", l), c), scalar2=smc(("conv_b", l), c), op0=ALU.mult, op1=ALU.add),
                      reads=[("xh", p2), ("xhalo", p2), "wsm", *AE], writes=[("xc", p3)])

            def stageAd(n):
                c, tg = n // 4, n % 4
                p2, p3 = n % 2, n % 4
                for tap in range(1, 4):
                    S.add("dve", lambda e, tap=tap: e.scalar_tensor_tensor(out=xc[p3][:], in0=xh[p2][:, tap:tap + 512], scalar=smc(("conv_w", l), tap * 8 + c), in1=xc[p3][:], op0=ALU.mult, op1=ALU.add),
                          reads=[("xh", p2), ("xhalo", p2), ("xc", p3), "wsm", *AE], writes=[("xc", p3)])
                S.add("dve", lambda e: e.tensor_copy(out=xcb[p2][:], in_=xc[p3][:]), reads=[("xc", p3), *AE], writes=[("xcb", p2)])

            def stageA2(n):
                c, tg = n // 4, n % 4
                p2 = n % 2
                bank_r = 4 + 2 * p2
                bank_i = bank_r + 1
                S.add("pe", lambda e: e.matmul(ps[:, bank_r, :], lhsT=wg[:, 0, c, :], rhs=xcb[p2][:], start=True, stop=True),
                      reads=["gate", ("xcb", p2), *AE], ex=[("ps", bank_r)])
                S.add("pe", lambda e: e.matmul(ps[:, bank_i, :], lhsT=wg[:, 1, c, :], rhs=xcb[p2][:], start=True, stop=True),
                      reads=["gate", ("xcb", p2), *AE], ex=[("ps", bank_i)])

            def stageB(n):
                c, tg = n // 4, n % 4
                p2, p3 = n % 2, n % 3
                bank_r = 4 + 2 * p2
                bank_i = bank_r + 1
                S.add("act", lambda e: e.activation(out=ab[p2][:], in_=ps[:, bank_r, :], func=AF.Tanh, scale=0.5, bias=rder[:, 16 + c:17 + c]),
                      reads=["rder", *AE], writes=[("ab", p2)], ex=[("ps", bank_r)])
                S.add("act", lambda e: e.activation(out=tb[p3][:], in_=ps[:, bank_i, :], func=AF.Tanh, scale=0.5, bias=rder[:, 24 + c:25 + c]),
                      reads=["rder", *AE], writes=[("tb", p3)], ex=[("ps", bank_i)])
                S.add("act", lambda e: e.activation(out=mb[p2][:], in_=ab[p2][:], func=AF.Exp, scale=rder[:, c:c + 1], bias=rder[:, c:c + 1]),
                      reads=[("ab", p2), "rder", *AE], writes=[("mb", p2)])
                S.add("act", lambda e: e.activation(out=ab[p2][:], in_=ab[p2][:], func=AF.Exp, scale=rder[:, 8 + c:9 + c], bias=rder[:, 8 + c:9 + c]),
                      reads=[("ab", p2), "rder", *AE], writes=[("ab", p2)])
                S.add("act", lambda e: e.activation(out=mb[p2][:], in_=mb[p2][:], func=AF.Ln, scale=-1.0, bias=1.0),
                      reads=[("mb", p2), *AE], writes=[("mb", p2)])
                S.add("act", lambda e: e.activation(out=mb[p2][:], in_=mb[p2][:], func=AF.Exp, scale=0.5),
                      reads=[("mb", p2), *AE], writes=[("mb", p2)])

            def stageC(n):
                c, tg = n // 4, n % 4
                p2, p3 = n % 2, n % 3
                p4 = n % 4
                q3 = (n - 1) % 3
                tsl = slice(tg * TG, (tg + 1) * TG)
                S.add("dve", lambda e: e.scalar_tensor_tensor(out=tb[p3][:], in0=tb[p3][:], scalar=1.0, in1=xc[p4][:], op0=ALU.add, op1=ALU.mult),
                      reads=[("tb", p3), ("xc", p4), *AE], writes=[("tb", p3)])
                S.add("dve", lambda e: e.scalar_tensor_tensor(out=mb[p2][:], in0=tb[p3][:], scalar=0.5, in1=mb[p2][:], op0=ALU.mult, op1=ALU.mult),
                      reads=[("tb", p3), ("mb", p2), *AE], writes=[("mb", p2)])
                if tg == 0:
                    S.add("dve", lambda e: e.tensor_tensor_scan(out=tb[p3][:], data0=ab[p2][:], data1=mb[p2][:], initial=0.0, op0=ALU.mult, op1=ALU.add),
                          reads=[("ab", p2), ("mb", p2), ("tb", p3), *AE], writes=[("tb", p3)])
                else:
                    S.add("dve", lambda e: e.tensor_tensor_scan(out=tb[p3][:], data0=ab[p2][:], data1=mb[p2][:], initial=tb[q3][:, 511:512], op0=ALU.mult, op1=ALU.add),
                          reads=[("ab", p2), ("mb", p2), ("tb", p3), ("tb", q3), *AE], writes=[("tb", p3)])
                S.add("pool", lambda e: e.tensor_tensor(out=oT[:, c, tsl], in0=tb[p3][:], in1=yb[c % 2][:, tsl], op=ALU.mult),
                      reads=[("tb", p3), ("yb", c % 2, tg), "AE_C", *AE], writes=[("oT", c, tg)])

            NIT = 32
            stageY(0)
            stageA(0)
            stageAd(0)
            stageA(1)
            stageAd(1)
            stageA2(0)
            stageA(2)
            stageAd(2)
            stageA2(1)
            stageB(0)
            for n in range(NIT):
                if n + 3 < NIT:
                    if (n + 3) % 4 == 0:
                        stageY((n + 3) // 4)
                    stageA(n + 3)
                if n + 2 < NIT:
                    stageA2(n + 2)
                if n + 1 < NIT:
                    stageB(n + 1)
                stageC(n)
                if n + 3 < NIT:
                    stageAd(n + 3)
            out_proj(oT, ("b_o", l))

        OT_OFF = ARENA_W - 8192
        SPEC_NORM = [("sq", (8, 512), BF16, 6144), ("rstd0", (512,), F32), ("rstd1", (512,), F32),
                     ("stmp0", (512,), F32), ("stmp1", (512,), F32)]
        SPEC_MLP = [("z", (4, 2048), BF16), ("rtmp", (4, 512), F32)]
        SPEC_ATT = [("oT", (8, 2048), BF16, OT_OFF), ("QT0", (2048,), BF16, 0), ("KA0", (2048,), BF16), ("KB0", (2048,), BF16), ("Vx0", (16, 192), BF16),
                    ("QT1", (2048,), BF16), ("KA1", (2048,), BF16), ("KB1", (2048,), BF16), ("Vx1", (16, 192), BF16),
                    ("expS0", (512,), BF16), ("expS1", (512,), BF16), ("expS2", (512,), BF16), ("expS3", (512,), BF16),
                    ("rc0", (512,), F32), ("rc1", (512,), F32), ("relb0", (2, 256), F32),
                    ("expB0", (2, 256), BF16), ("expB1", (2, 256), BF16)]
        SPEC_REC = [("oT", (8, 2048), BF16, OT_OFF), ("yb0", (2048,), BF16, 0), ("yb1", (2048,), BF16), ("xh0", (516,), F32), ("xh1", (516,), F32),
                    ("xc0", (512,), F32), ("xc1", (512,), F32), ("xc2", (512,), F32), ("xc3", (512,), F32), ("xcb0", (512,), BF16), ("xcb1", (512,), BF16),
                    ("ab0", (512,), F32), ("ab1", (512,), F32), ("mb0", (512,), F32), ("mb1", (512,), F32),
                    ("tb0", (512,), F32), ("tb1", (512,), F32), ("tb2", (512,), F32)]

        for l in layers:
            if not DEBUG.get("skip_mix"):
                carve(SPEC_NORM)
                rmsnorm(("gmix", l))
                if l % 2 == 0:
                    carve(SPEC_ATT)
                    attention(l)
                else:
                    carve(SPEC_REC)
                    recurrent(l)
            else:
                for _ in range(10):
                    wnext(); wdone()
            if not DEBUG.get("skip_mlp"):
                carve(SPEC_NORM)
                rmsnorm(("gmlp", l))
                carve(SPEC_MLP)
                mlp()
            else:
                for _ in range(16):
                    wnext(); wdone()
        if final:
            carve(SPEC_NORM)
            rmsnorm("gfin", to_h=True)
        for tg in range(NTG):
            for hf in range(2):
                S.add("sp", lambda e, tg=tg, hf=hf: e.dma_start(out=outT_v[:, hf * 4:(hf + 1) * 4, tg * TG:(tg + 1) * TG], in_=h[:, hf * 4:(hf + 1) * 4, tg * TG:(tg + 1) * TG]),
                      reads=[("h", dc, tg) for dc in range(hf * 4, (hf + 1) * 4)], chan="hout%d" % (tg * 2 + hf))
        assert state["next_use"] == n_pieces, (state["next_use"], n_pieces)
        S.finalize(st)
        with nc.Block() as block:
            S.emit(block)
    return nc


_CACHE = {}


def run_layers(inp, x, layers, final, n_cores=None):
    B = x.shape[0]
    wbig, wsm, relb, wgate, sm_off = prep_weights(inp, layers, final)
    nc = build_program(layers, final, wbig.shape[0], wsm.shape[1], relb.shape[0], wgate.shape[0], sm_off)
    in_maps = []
    for b in range(B):
        in_maps.append({"xT": np.ascontiguousarray(x[b].T), "wbig": wbig, "wsm": wsm, "relb": relb, "wgate": wgate})
    res = run_bass_kernel_spmd(nc, in_maps, core_ids=list(range(B)))
    out = np.stack([np.ascontiguousarray(r["outT"].T) for r in res.results], axis=0)
    return out


def kernel(**inputs):
    inp = {k: np.asarray(v, dtype=np.float32) for k, v in inputs.items()}
    x = inp["x"]
    out = run_layers(inp, x, [0, 1, 2, 3], True)
    return out.astype(np.float32)
```

```python
from contextlib import ExitStack
import numpy as np
import concourse.bass as bass
import concourse.mybir as mybir
from concourse.bass_utils import run_bass_kernel_spmd

F32 = mybir.dt.float32
BF16 = mybir.dt.bfloat16
AF = mybir.ActivationFunctionType
ALU = mybir.AluOpType

D = 1024
SEQ = 2048
NTG = 4
TG = 512
NDC = 8
NSLOT = 3
SLOT_N = 4096
NEG = -30000.0
EPS = 1e-6
DEBUG = {}


class _Op:
    __slots__ = ("idx", "eng", "fn", "deps", "chan", "sig", "waits", "has_dep")


class Sched:
    def __init__(self, nc, same_engine_sync=True):
        self.nc = nc
        self.ops = []
        self.last_writer = {}
        self.readers = {}
        self.same_engine_sync = same_engine_sync

    def add(self, eng, fn, reads=(), writes=(), ex=(), chan=None):
        op = _Op()
        op.idx = len(self.ops)
        op.eng = eng
        op.fn = fn
        op.chan = chan
        op.sig = None
        op.has_dep = False
        reads = list(reads)
        writes = list(writes) + list(ex)
        deps = set()
        for k in reads:
            w = self.last_writer.get(k)
            if w is not None:
                deps.add(w)
        for k in writes:
            w = self.last_writer.get(k)
            if w is not None:
                deps.add(w)
            rd = self.readers.get(k)
            if rd:
                for v in rd.values():
                    if isinstance(v, list):
                        deps.update(v)
                    else:
                        deps.add(v)
        for k in reads:
            rd = self.readers.setdefault(k, {})
            if chan is not None:
                rd.setdefault(("dma", eng), []).append(op.idx)
            else:
                rd[eng] = op.idx
        for k in writes:
            self.last_writer[k] = op.idx
            self.readers[k] = {}
        deps.discard(op.idx)
        op.deps = deps
        self.ops.append(op)
        return op

    def finalize(self, stack):
        nc = self.nc
        ops = self.ops
        for op in ops:
            keep = set()
            for d in op.deps:
                dop = ops[d]
                if dop.chan is None and op.chan is None and dop.eng == op.eng:
                    if dop.eng == "pe" or not self.same_engine_sync:
                        continue
                keep.add(d)
            op.deps = keep
            for d in keep:
                ops[d].has_dep = True
        sems = {}
        counts = {}

        def get_sem(name):
            if name not in sems:
                sems[name] = stack.enter_context(nc.semaphore("s_" + name))
                counts[name] = 0

        for op in ops:
            if op.chan is not None:
                nm = "c_" + op.chan
                get_sem(nm)
                counts[nm] += 16
                op.sig = (nm, counts[nm])
            elif op.has_dep:
                nm = "e_" + op.eng
                get_sem(nm)
                counts[nm] += 1
                op.sig = (nm, counts[nm])
        waited = {}
        for op in ops:
            need = {}
            for d in op.deps:
                s, v = ops[d].sig
                if v > need.get(s, 0):
                    need[s] = v
            w = waited.setdefault(op.eng, {})
            op.waits = []
            for s, v in need.items():
                if w.get(s, 0) < v:
                    w[s] = v
                    op.waits.append((s, v))
        self.sems = sems
        self.counts = counts
        per_eng = {}
        for op in ops:
            per_eng.setdefault(op.eng, []).append(op)
        self.per_eng = per_eng

    def emit(self, block):
        sems = self.sems
        per_eng = self.per_eng
        counts = self.counts

        def run(engname, e):
            for op in per_eng.get(engname, []):
                for s, v in op.waits:
                    e.wait_ge(sems[s], v)
                ins = op.fn(e)
                if op.sig is not None:
                    ins.then_inc(sems[op.sig[0]], 16 if op.chan is not None else 1)
            if engname == "sp":
                for s, v in counts.items():
                    if v > 0:
                        e.wait_ge(sems[s], v)

        @block.sync
        def _(e):
            run("sp", e)

        @block.tensor
        def _(e):
            run("pe", e)

        @block.scalar
        def _(e):
            run("act", e)

        @block.vector
        def _(e):
            run("dve", e)

        @block.gpsimd
        def _(e):
            run("pool", e)


def _piece(w_sub):
    k, cols = w_sub.shape
    nch = k // 128
    out = np.ascontiguousarray(w_sub.reshape(nch, 128, cols).transpose(1, 0, 2)).reshape(128, nch * cols)
    if out.shape[1] < SLOT_N:
        pad = np.zeros((128, SLOT_N - out.shape[1]), np.float32)
        out = np.concatenate([out, pad], axis=1)
    return out


def _col(v):
    return np.ascontiguousarray(v.reshape(-1, 128).T)


def layer_kinds(layers):
    return ["attn" if l % 2 == 0 else "rec" for l in layers]


def prep_weights(inp, layers, final):
    pieces = []
    sm = []
    sm_off = {}

    def add_sm(name, arr):
        sm_off[name] = sum(a.shape[1] for a in sm)
        sm.append(np.ascontiguousarray(arr, dtype=np.float32))

    relb = []
    gates = []
    maskc = np.zeros((128, 256), np.float32)
    maskc[64:128, 0:64] = NEG
    add_sm("maskc", maskc)
    kq = np.arange(128)
    rel0 = kq[None, :] - kq[:, None]
    idx_d0 = np.clip(rel0, -128, 128) + 128
    idx_d1 = np.clip(rel0 + 128, -128, 128) + 128
    for l in layers:
        j = l // 2
        add_sm(("gmix", l), _col(inp["norm_mix"][l]))
        add_sm(("gmlp", l), _col(inp["norm_mlp"][l]))
        if l % 2 == 0:
            wqkv = inp["attn_w_qkv"][j]
            for p in range(8):
                sub = np.concatenate([wqkv[:, p * 128:(p + 1) * 128],
                                      wqkv[:, 1024 + p * 128:1024 + (p + 1) * 128],
                                      wqkv[:, 2048 + p * 128:2048 + (p + 1) * 128]], axis=1)
                pieces.append(_piece(sub))
                tab = inp["attn_rel_bias"][j]
                rb = np.zeros((128, 512), np.float32)
                for hh in range(2):
                    t = tab[2 * p + hh]
                    rb[:, hh * 256:hh * 256 + 128] = t[idx_d0]
                    rb[:, hh * 256 + 128:hh * 256 + 256] = t[idx_d1]
                relb.append(rb)
            wo = inp["attn_w_o"][j]
            for hf in range(2):
                pieces.append(_piece(wo[:, hf * 512:(hf + 1) * 512]))
            add_sm(("chcol", l), np.broadcast_to(inp["attn_rel_bias"][j][:, 256].reshape(1, 16), (128, 16)))
        else:
            ga = inp["rec_w_ga"][j]
            gx = inp["rec_w_gx"][j]
            gp = np.concatenate([ga.transpose(1, 0, 2).reshape(128, 1024), gx.transpose(1, 0, 2).reshape(128, 1024)], axis=1)
            gates.append(np.ascontiguousarray(gp))
            win = inp["rec_w_in"][j]
            for c in range(8):
                sub = np.concatenate([win[:, c * 128:(c + 1) * 128], win[:, 1024 + c * 128:1024 + (c + 1) * 128]], axis=1)
                pieces.append(_piece(sub))
            wo = inp["rec_w_o"][j]
            for hf in range(2):
                pieces.append(_piece(wo[:, hf * 512:(hf + 1) * 512]))
            add_sm(("b_in", l), _col(inp["rec_b_in"][j]))
            add_sm(("conv_w", l), np.concatenate([_col(inp["rec_conv_w"][j][t]) for t in range(4)], axis=1))
            add_sm(("conv_b", l), _col(inp["rec_conv_b"][j]))
            add_sm(("b_ga", l), _col(inp["rec_b_ga"][j].reshape(-1)))
            add_sm(("b_gx", l), _col(inp["rec_b_gx"][j].reshape(-1)))
            add_sm(("a_param", l), _col(inp["rec_a_param"][j]))
            add_sm(("b_o", l), _col(inp["rec_b_o"][j]))
        w1 = inp["mlp_w1"][l]
        w2 = inp["mlp_w2"][l]
        for g in range(8):
            pieces.append(_piece(w1[:, g * 512:(g + 1) * 512]))
            pieces.append(_piece(w2[g * 512:(g + 1) * 512, :]))
    if final:
        add_sm("gfin", _col(inp["norm_final"]))
    wbig = np.ascontiguousarray(np.stack(pieces, axis=0))
    wsm = np.ascontiguousarray(np.concatenate(sm, axis=1))
    if not relb:
        relb = [np.zeros((128, 512), np.float32)]
    relb = np.ascontiguousarray(np.stack(relb, axis=0))
    if not gates:
        gates = [np.zeros((128, 2048), np.float32)]
    wgate = np.ascontiguousarray(np.stack(gates, axis=0))
    return wbig, wsm, relb, wgate, sm_off


def build_program(layers, final, n_pieces, n_sm, n_relb, n_gate, sm_off):
    nc = bass.Bass("TRN2", target_bir_lowering=False)
    xT = nc.dram_tensor("xT", [D, SEQ], F32, kind="ExternalInput").ap()
    wbig = nc.dram_tensor("wbig", [n_pieces, 128, SLOT_N], F32, kind="ExternalInput").ap()
    wsm_d = nc.dram_tensor("wsm", [128, n_sm], F32, kind="ExternalInput").ap()
    relb_d = nc.dram_tensor("relb", [n_relb, 128, 512], F32, kind="ExternalInput").ap()
    wgate_d = nc.dram_tensor("wgate", [n_gate, 128, 2048], F32, kind="ExternalInput").ap()
    outT = nc.dram_tensor("outT", [D, SEQ], F32, kind="ExternalOutput").ap()
    xT_v = xT.rearrange("(c p) t -> p c t", p=128)
    outT_v = outT.rearrange("(c p) t -> p c t", p=128)

    with ExitStack() as st:
        def sb(name, shape, dt):
            return st.enter_context(nc.sbuf_tensor("k_" + name, shape, dt))

        h = sb("h", [128, NDC, SEQ], F32)
        hn = sb("hn", [128, NDC, SEQ], BF16)
        wsl = sb("wsl", [128, NSLOT, SLOT_N], BF16)
        wsm = sb("wsm_sb", [128, n_sm], F32)
        ones = sb("ones", [128, 128], BF16)
        dummy = sb("dummy", [128, 8], F32)
        rder = sb("rder", [128, 40], F32)
        gatebuf = sb("gatebuf", [128, 2048], BF16)
        ARENA_W = 20480
        arena = sb("arena", [128, ARENA_W], F32)
        ps = st.enter_context(nc.psum_tensor("k_ps", [128, 8, 512], F32))
        arena_t = {}

        def carve(spec):
            arena_t.clear()
            off = 0
            for ent in spec:
                name, shape, dt = ent[0], ent[1], ent[2]
                if len(ent) > 3:
                    off = ent[3]
                n = int(np.prod(shape))
                words = n // 2 if dt == BF16 else n
                v = arena[:, off:off + words]
                if dt == BF16:
                    v = v.bitcast(BF16)
                if len(shape) == 2:
                    v = v.rearrange("p (a b) -> p a b", b=shape[1])
                elif len(shape) == 3:
                    v = v.rearrange("p (a b c) -> p a b c", b=shape[1], c=shape[2])
                arena_t[name] = v
                off += words
            assert off <= ARENA_W, (off, ARENA_W)

        S = Sched(nc)
        AE = ["AE_A", "AE_B", "AE_C"]

        def arena_join(keys):
            S.add("pool", lambda e: e.memset(dummy[:, 0:1], 0.0), writes=list(keys) + ["dummy"])

        state = {"next_load": 0, "next_use": 0}

        def wload():
            k = state["next_load"]
            if k >= n_pieces:
                return
            state["next_load"] += 1
            slot = k % NSLOT
            src = wbig[k].rearrange("p (a b) -> p a b", b=1024)
            dst = wsl[:, slot, :].rearrange("p (a b) -> p a b", b=1024)
            S.add("pool", lambda e: e.dma_start(out=dst, in_=src), writes=[("w", slot)], chan="w%d" % slot)

        def wnext():
            k = state["next_use"]
            state["next_use"] += 1
            assert k < state["next_load"], "piece used before load emitted"
            return k % NSLOT

        def wdone():
            wload()

        def smc(name, i=0, n=1):
            o = sm_off[name] + i
            return wsm[:, o:o + n]

        S.add("sp", lambda e: e.dma_start(out=wsm[:], in_=wsm_d[:, :]), writes=["wsm"], chan="wsm")
        for tg in range(NTG):
            for hf in range(2):
                S.add("sp", lambda e, tg=tg, hf=hf: e.dma_start(out=h[:, hf * 4:(hf + 1) * 4, tg * TG:(tg + 1) * TG], in_=xT_v[:, hf * 4:(hf + 1) * 4, tg * TG:(tg + 1) * TG]),
                      writes=[("h", dc, tg) for dc in range(hf * 4, (hf + 1) * 4)], chan="hin%d" % (tg * 2 + hf))
        S.add("dve", lambda e: e.memset(ones[:], 1.0), writes=["ones"])
        for _ in range(NSLOT):
            wload()

        def rmsnorm(gname, to_h=False):
            AE[:] = ["AE_B"]
            arena_join(["AE_B"])
            sq = arena_t["sq"]
            rstd = [arena_t["rstd0"], arena_t["rstd1"]]
            stmp = [arena_t["stmp0"], arena_t["stmp1"]]
            for tg in range(NTG):
                par = tg % 2
                tsl = slice(tg * TG, (tg + 1) * TG)
                bank = par
                for dc in range(NDC):
                    S.add("act", lambda e, dc=dc, tsl=tsl: e.activation(out=sq[:, dc, :], in_=h[:, dc, tsl], func=AF.Square),
                          reads=[("h", dc, tg), *AE], writes=[("sq", dc)])
                for dc in range(NDC):
                    S.add("pe", lambda e, dc=dc, bank=bank: e.matmul(ps[:, bank, :], lhsT=ones[:], rhs=sq[:, dc, :], start=(dc == 0), stop=(dc == NDC - 1)),
                          reads=[("sq", dc), "ones", *AE], ex=[("ps", bank)])
                if to_h:
                    S.add("act", lambda e, par=par, bank=bank: e.activation(out=stmp[par][:], in_=ps[:, bank, :], func=AF.Sqrt, scale=1.0 / D, bias=EPS),
                          reads=[*AE], writes=[("stmp", par)], ex=[("ps", bank)])
                    S.add("dve", lambda e, par=par: e.reciprocal(out=rstd[par][:], in_=stmp[par][:]),
                          reads=[("stmp", par), *AE], writes=[("rstd", par)])
                else:
                    S.add("act", lambda e, par=par, bank=bank: e.activation(out=stmp[par][:], in_=ps[:, bank, :], func=AF.Ln, scale=1.0 / D, bias=EPS),
                          reads=[*AE], writes=[("stmp", par)], ex=[("ps", bank)])
                    S.add("act", lambda e, par=par: e.activation(out=rstd[par][:], in_=stmp[par][:], func=AF.Exp, scale=-0.5),
                          reads=[("stmp", par), *AE], writes=[("rstd", par)])
                for dc in range(NDC):
                    if to_h:
                        S.add("dve", lambda e, dc=dc, par=par, tsl=tsl: e.scalar_tensor_tensor(out=h[:, dc, tsl], in0=h[:, dc, tsl], scalar=smc(gname, dc), in1=rstd[par][:], op0=ALU.mult, op1=ALU.mult),
                              reads=[("h", dc, tg), ("rstd", par), "wsm", *AE], writes=[("h", dc, tg)])
                    else:
                        S.add("dve", lambda e, dc=dc, par=par, tsl=tsl: e.scalar_tensor_tensor(out=hn[:, dc, tsl], in0=h[:, dc, tsl], scalar=smc(gname, dc), in1=rstd[par][:], op0=ALU.mult, op1=ALU.mult),
                              reads=[("h", dc, tg), ("rstd", par), "wsm", *AE], writes=[("hn", dc, tg)])

        def mlp():
            AE[:] = ["AE_A"]
            arena_join(["AE_A"])
            z = arena_t["z"]
            rtmp = arena_t["rtmp"]
            cnt = {"a": 0, "b": 0}
            for g in range(8):
                s1 = wnext()
                w1v = wsl[:, s1, :].rearrange("p (c f) -> p c f", f=512)
                for tg in range(NTG):
                    tsl = slice(tg * TG, (tg + 1) * TG)
                    for fc in range(4):
                        bank = cnt["a"] % 4
                        rt = cnt["a"] % 4
                        cnt["a"] += 1
                        for dc in range(NDC):
                            S.add("pe", lambda e, dc=dc, fc=fc, bank=bank, tsl=tsl, w1v=w1v: e.matmul(ps[:, bank, :], lhsT=w1v[:, dc, fc * 128:(fc + 1) * 128], rhs=hn[:, dc, tsl], start=(dc == 0), stop=(dc == NDC - 1)),
                                  reads=[("w", s1), ("hn", dc, tg)], ex=[("ps", bank)])
                        S.add("act", lambda e, bank=bank, rt=rt: e.activation(out=rtmp[:, rt, :], in_=ps[:, bank, :], func=AF.Relu),
                              reads=[*AE], writes=[("rtmp", rt)], ex=[("ps", bank)])
                        S.add("act", lambda e, rt=rt, fc=fc, tsl=tsl: e.activation(out=z[:, fc, tsl], in_=rtmp[:, rt, :], func=AF.Square),
                              reads=[("rtmp", rt), *AE], writes=[("z", fc, tg)])
                wdone()
                s2 = wnext()
                w2v = wsl[:, s2, :].rearrange("p (c f) -> p c f", f=1024)
                for tg in range(NTG):
                    tsl = slice(tg * TG, (tg + 1) * TG)
                    for dcs in range(NDC):
                        bank = 4 + cnt["b"] % 4
                        cnt["b"] += 1
                        for fc in range(4):
                            S.add("pe", lambda e, fc=fc, dcs=dcs, bank=bank, tsl=tsl, w2v=w2v: e.matmul(ps[:, bank, :], lhsT=w2v[:, fc, dcs * 128:(dcs + 1) * 128], rhs=z[:, fc, tsl], start=(fc == 0), stop=(fc == 3)),
                                  reads=[("w", s2), ("z", fc, tg), *AE], ex=[("ps", bank)])
                        S.add("dve", lambda e, dcs=dcs, bank=bank, tsl=tsl: e.tensor_tensor(out=h[:, dcs, tsl], in0=ps[:, bank, :], in1=h[:, dcs, tsl], op=ALU.add),
                              reads=[("h", dcs, tg)], writes=[("h", dcs, tg)], ex=[("ps", bank)])
                wdone()

        def out_proj(oT, bias_name=None):
            AE[:] = ["AE_C"]
            cnt = 0
            ss = [wnext(), wnext()]
            wvs = [wsl[:, sx, :].rearrange("p (c f) -> p c f", f=512) for sx in ss]
            for tg in range(NTG):
                tsl = slice(tg * TG, (tg + 1) * TG)
                for hf in range(2):
                    s_, wv = ss[hf], wvs[hf]
                    for dcs in range(4):
                        dco = hf * 4 + dcs
                        bank = 6 + cnt % 2
                        cnt += 1
                        for pc in range(NDC):
                            S.add("pe", lambda e, pc=pc, dcs=dcs, bank=bank, tsl=tsl, wv=wv: e.matmul(ps[:, bank, :], lhsT=wv[:, pc, dcs * 128:(dcs + 1) * 128], rhs=oT[:, pc, tsl], start=(pc == 0), stop=(pc == NDC - 1)),
                                  reads=[("w", s_), ("oT", pc, tg), *AE], ex=[("ps", bank)])
                        if bias_name is None:
                            S.add("dve", lambda e, dco=dco, bank=bank, tsl=tsl: e.tensor_tensor(out=h[:, dco, tsl], in0=ps[:, bank, :], in1=h[:, dco, tsl], op=ALU.add),
                                  reads=[("h", dco, tg)], writes=[("h", dco, tg)], ex=[("ps", bank)])
                        else:
                            S.add("dve", lambda e, dco=dco, bank=bank, tsl=tsl: e.scalar_tensor_tensor(out=h[:, dco, tsl], in0=ps[:, bank, :], scalar=smc(bias_name, dco), in1=h[:, dco, tsl], op0=ALU.add, op1=ALU.add),
                                  reads=[("h", dco, tg), "wsm"], writes=[("h", dco, tg)], ex=[("ps", bank)])
            wdone()
            wdone()

        relb_state = {"n": 0}

        def attention(l):
            AE[:] = ["AE_A", "AE_B"]
            arena_join(["AE_A", "AE_B", "AE_C"])
            oT = arena_t["oT"]
            QT = [arena_t["QT0"], arena_t["QT1"]]
            KAB = [[arena_t["KA0"], arena_t["KB0"]], [arena_t["KA1"], arena_t["KB1"]]]
            expB = [arena_t["expB0"], arena_t["expB1"]]
            Vx = [arena_t["Vx0"], arena_t["Vx1"]]
            expS = [arena_t["expS%d" % i] for i in range(4)]
            rc = [arena_t["rc0"], arena_t["rc1"]]
            rbuf = [arena_t["relb0"], arena_t["relb0"]]
            Bm = arena_t["relb0"]
            for b in range(2):
                S.add("pool", lambda e, b=b: e.memset(Vx[b][:, :, 64:128], 1.0), reads=[*AE], writes=[("Vx_ones", b)])
                S.add("pool", lambda e, b=b: e.memset(KAB[b][1][0:64, :], 0.0), reads=[*AE], writes=[("Kz", b)])
            cnts = {"ss0": 0, "ss1": 0, "po": 0, "pj": 0}
            PJ_BANKS = [6, 7]

            def load_relb(p):
                gi = relb_state["n"]
                relb_state["n"] += 1
                S.add("sp", lambda e, gi=gi, p=p: e.dma_start(out=rbuf[p % 2][:].rearrange("p a b -> p (a b)"), in_=relb_d[gi]), reads=[*AE], writes=[("relb", 0)], chan="relb0")

            def proj_units(p, banks):
                pb = p % 2
                s = wnext()
                wv = wsl[:, s, 0:8 * 384].rearrange("p (c f) -> p c f", f=384)
                units = []
                for which in (0, 1):
                    for tg in range(NTG):
                        ubank = {}

                        def ua(which=which, tg=tg, ubank=ubank):
                            tsl = slice(tg * TG, (tg + 1) * TG)
                            bank = banks[cnts["pj"] % len(banks)]
                            cnts["pj"] += 1
                            ubank[0] = bank
                            for dc in range(NDC // 2):
                                S.add("pe", lambda e, dc=dc: e.matmul(ps[:, bank, :], lhsT=wv[:, dc, which * 128:(which + 1) * 128], rhs=hn[:, dc, tsl], start=(dc == 0), stop=False),
                                      reads=[("w", s), ("hn", dc, tg)], ex=[("ps", bank)])
                        units.append(ua)

                        def u(which=which, tg=tg, ubank=ubank):
                            tsl = slice(tg * TG, (tg + 1) * TG)
                            bank = ubank[0]
                            for dc in range(NDC // 2, NDC):
                                S.add("pe", lambda e, dc=dc: e.matmul(ps[:, bank, :], lhsT=wv[:, dc, which * 128:(which + 1) * 128], rhs=hn[:, dc, tsl], start=False, stop=(dc == NDC - 1)),
                                      reads=[("w", s), ("hn", dc, tg)], ex=[("ps", bank)])
                            if which == 0:
                                S.add("dve", lambda e: e.tensor_scalar_mul(out=QT[pb][:, tsl], in0=ps[:, bank, :], scalar1=0.125),
                                      reads=[*AE], writes=[("qk", pb, 0, tg)], ex=[("ps", bank)])
                            else:
                                S.add("dve", lambda e: e.tensor_copy(out=KAB[pb][0][:, tsl], in_=ps[:, bank, :]),
                                      reads=[*AE], writes=[("qk", pb, 1, tg)], ex=[("ps", bank)])
                                S.add("pool", lambda e: e.tensor_copy(out=KAB[pb][1][64:128, tsl], in_=KAB[pb][0][64:128, tsl]),
                                      reads=[("qk", pb, 1, tg), *AE], writes=[("qk", pb, 2, tg)])
                                S.add("pool", lambda e: e.memset(KAB[pb][0][64:128, tsl], 0.0),
                                      reads=[*AE], writes=[("qk", pb, 1, tg)])
                        units.append(u)
                vbank = {}
                for t in range(16):
                    def u(t=t):
                        tq, t4 = t // 4, t % 4
                        if t4 == 0:
                            vbank[tq] = banks[cnts["pj"] % len(banks)]
                            cnts["pj"] += 1
                        bank = vbank[tq]
                        for dc in range(NDC):
                            S.add("pe", lambda e, dc=dc: e.matmul(ps[:, bank, t4 * 128:(t4 + 1) * 128], lhsT=hn[:, dc, t * 128:(t + 1) * 128], rhs=wv[:, dc, 256:384], start=(dc == 0), stop=(dc == NDC - 1)),
                                  reads=[("w", s), ("hn", dc, t // 4)], ex=[("ps", bank)])
                        if t4 == 3:
                            psv = ps[:, bank, :].rearrange("p (t c) -> p t c", c=128)
                            S.add("dve", lambda e: e.tensor_copy(out=Vx[pb][:, tq * 4:(tq + 1) * 4, 0:64], in_=psv[:, :, 0:64]),
                                  reads=[*AE], writes=[("V", pb, tq, 0)], ex=[("ps", bank)])
                            S.add("dve", lambda e: e.tensor_copy(out=Vx[pb][:, tq * 4:(tq + 1) * 4, 128:192], in_=psv[:, :, 64:128]),
                                  reads=[*AE], writes=[("V", pb, tq, 1)], ex=[("ps", bank)])
                    units.append(u)
                units.append(wdone)
                return units

            load_relb(0)
            for u in proj_units(0, [0, 1, 2, 3, 7]):
                u()
            for p in range(8):
                pb = p % 2
                if p + 1 < 8:
                    units = proj_units(p + 1, PJ_BANKS)
                else:
                    units = []
                for hh in range(2):
                    S.add("dve", lambda e, hh=hh, p=p: e.scalar_tensor_tensor(out=Bm[:, hh, :], in0=Bm[:, hh, :], scalar=smc(("chcol", l), 2 * p + hh), in1=smc("maskc", 0, 256), op0=ALU.subtract, op1=ALU.add),
                          reads=["wsm", *AE], writes=[("relb", 0)])
                    S.add("act", lambda e, hh=hh, p=p: e.activation(out=expB[p % 2][:, hh, :], in_=Bm[:, hh, :], func=AF.Exp),
                          reads=[("relb", 0), *AE], writes=[("expB", p % 2, hh)])
                if p + 1 < 8:
                    load_relb(p + 1)
                items = []
                for G in range(4):
                    for t in range(max(0, 4 * G - 4), 4 * G + 4):
                        for hh in range(2):
                            items.append((G, t, hh))
                nitems = len(items)
                ntot = max(len(units), 1)
                sset_of = {}

                def geom(G, t):
                    j0 = max(t, 4 * G)
                    j1 = min(t + 4, 4 * G + 3)
                    return j0, j1, (j1 - j0 + 1) * 128

                def scores(item, p=p, pb=pb):
                    G, t, hh = item
                    r0 = hh * 64
                    j0, j1, N = geom(G, t)
                    k = cnts["ss0"]
                    cnts["ss0"] += 1
                    es = expS[k % 4]
                    ek = k % 4
                    sset_of[item] = (es, ek)
                    bank = k % 2
                    S.add("pe", lambda e: e.matmul(ps[:, bank, 0:N], lhsT=KAB[pb][hh][:, t * 128:(t + 1) * 128], rhs=QT[pb][:, j0 * 128:(j1 + 1) * 128], start=True, stop=True),
                          reads=[("qk", pb, 0, G), ("qk", pb, 1, t // 4), ("qk", pb, 2, t // 4), ("Kz", pb), *AE], ex=[("ps", bank)])
                    S.add("act", lambda e: e.activation(out=es[:, 0:N], in_=ps[:, bank, 0:N], func=AF.Exp),
                          reads=[*AE], writes=[("expS", ek, 0), ("expS", ek, 1)], ex=[("ps", bank)])
                    if j1 == t + 4:
                        S.add("pool", lambda e: e.memset(es[0:64, N - 64:N], 0.0), reads=[*AE], writes=[("expS", ek, 1)])
                    d0 = j0 - t
                    if d0 == 0:
                        nn = min(2, (j1 - j0 + 1))
                        S.add("dve", lambda e: e.tensor_tensor(out=es[:, 0:nn * 128], in0=es[:, 0:nn * 128], in1=expB[p % 2][:, hh, 0:nn * 128], op=ALU.mult),
                              reads=[("expB", p % 2, hh), *AE], writes=[("expS", ek, 0)])
                    elif d0 == 1:
                        S.add("dve", lambda e: e.tensor_tensor(out=es[:, 0:128], in0=es[:, 0:128], in1=expB[p % 2][:, hh, 128:256], op=ALU.mult),
                              reads=[("expB", p % 2, hh), *AE], writes=[("expS", ek, 0)])

                def pv(item, p=p, pb=pb):
                    G, t, hh = item
                    j0, j1, N = geom(G, t)
                    es, ek = sset_of[item]
                    vcols = slice(0, 128) if hh == 0 else slice(64, 192)
                    so = slice(64, 128) if hh == 0 else slice(0, 64)
                    oo = slice(0, 64) if hh == 0 else slice(64, 128)
                    bank = 2 + 2 * hh + G % 2
                    c0 = (j0 - 4 * G) * 128
                    first = (t == max(0, 4 * G - 4))
                    last = (t == 4 * G + 3)
                    S.add("pe", lambda e: e.matmul(ps[:, bank, c0:c0 + N], lhsT=Vx[pb][:, t, vcols], rhs=es[:, 0:N], start=first, stop=last, skip_group_check=True),
                          reads=[("expS", ek, 0), ("expS", ek, 1), ("V", pb, t // 4, 0), ("V", pb, t // 4, 1), ("Vx_ones", pb), *AE], ex=[("ps", bank)])
                    if last:
                        par = cnts["po"] % 2
                        cnts["po"] += 1
                        tsl = slice(G * TG, (G + 1) * TG)
                        S.add("act", lambda e: e.activation(out=rc[par][so, :], in_=ps[so, bank, :], func=AF.Ln),
                              reads=[*AE], writes=[("rc", par)], ex=[("ps", bank)])
                        S.add("act", lambda e: e.activation(out=rc[par][so, :], in_=rc[par][so, :], func=AF.Exp, scale=-1.0),
                              reads=[("rc", par), *AE], writes=[("rc", par)])
                        S.add("dve", lambda e: e.tensor_tensor(out=oT[oo, p, tsl], in0=ps[oo, bank, :], in1=rc[par][so, :], op=ALU.mult),
                              reads=[("rc", par), "AE_C", *AE], writes=[("oT", p, G)], ex=[("ps", bank)])

                LA = 3
                for i in range(nitems + LA):
                    if i < nitems:
                        scores(items[i])
                    if i >= LA:
                        pv(items[i - LA])
                        step = i - LA + 1
                        while units and len(units) * nitems > (nitems - step) * ntot:
                            units.pop(0)()
                while units:
                    units.pop(0)()
            out_proj(oT, None)

        gate_state = {"n": 0}

        def recurrent(l):
            AE[:] = ["AE_A", "AE_B"]
            arena_join(["AE_A", "AE_B", "AE_C"])
            oT = arena_t["oT"]
            yb = [arena_t["yb0"], arena_t["yb1"]]
            xh = [arena_t["xh0"], arena_t["xh1"]]
            xc = [arena_t["xc0"], arena_t["xc1"], arena_t["xc2"], arena_t["xc3"]]
            xcb = [arena_t["xcb0"], arena_t["xcb1"]]
            ab = [arena_t["ab0"], arena_t["ab1"]]
            mb = [arena_t["mb0"], arena_t["mb1"]]
            tb = [arena_t["tb0"], arena_t["tb1"], arena_t["tb2"]]
            S.add("act", lambda e: e.activation(out=rder[:, 32:40], in_=smc(("a_param", l), 0, 8), func=AF.Exp, scale=-1.0),
                  reads=["wsm"], writes=["rder_t"])
            S.add("act", lambda e: e.activation(out=rder[:, 32:40], in_=rder[:, 32:40], func=AF.Ln, bias=1.0),
                  reads=["rder_t"], writes=["rder_t"])
            S.add("dve", lambda e: e.tensor_scalar_mul(out=rder[:, 0:8], in0=rder[:, 32:40], scalar1=-8.0), reads=["rder_t"], writes=["rder"])
            S.add("dve", lambda e: e.tensor_scalar_mul(out=rder[:, 8:16], in0=rder[:, 32:40], scalar1=-4.0), reads=["rder_t"], writes=["rder"])
            S.add("dve", lambda e: e.tensor_scalar_mul(out=rder[:, 16:24], in0=smc(("b_ga", l), 0, 8), scalar1=0.5), reads=["wsm"], writes=["rder"])
            S.add("dve", lambda e: e.tensor_scalar_mul(out=rder[:, 24:32], in0=smc(("b_gx", l), 0, 8), scalar1=0.5), reads=["wsm"], writes=["rder"])
            gi = gate_state["n"]
            gate_state["n"] += 1
            S.add("pool", lambda e, gi=gi: e.dma_start(out=gatebuf[:].rearrange("p (a b) -> p a b", b=1024), in_=wgate_d[gi].rearrange("p (a b) -> p a b", b=1024)),
                  writes=["gate"], chan="gate")
            wg = gatebuf[:].rearrange("p (g n e) -> p g n e", g=2, n=8)
            cnt = {"pj": 0}
            wstate = {}

            def wv_of(c):
                if c not in wstate:
                    s = wnext()
                    wstate[c] = (s, wsl[:, s, 0:2048].rearrange("p (c f) -> p c f", f=256))
                return wstate[c]

            def stageY(c):
                s, wv = wv_of(c)
                for tg in range(NTG):
                    tsl = slice(tg * TG, (tg + 1) * TG)
                    bank = cnt["pj"] % 4
                    cnt["pj"] += 1
                    for dc in range(NDC):
                        S.add("pe", lambda e, dc=dc, bank=bank, tsl=tsl: e.matmul(ps[:, bank, :], lhsT=wv[:, dc, 128:256], rhs=hn[:, dc, tsl], start=(dc == 0), stop=(dc == NDC - 1)),
                              reads=[("w", s), ("hn", dc, tg)], ex=[("ps", bank)])
                    S.add("act", lambda e, bank=bank, tsl=tsl: e.activation(out=yb[c % 2][:, tsl], in_=ps[:, bank, :], func=AF.Gelu_apprx_tanh, bias=smc(("b_in", l), 8 + c)),
                          reads=["wsm", *AE], writes=[("yb", c % 2, tg)], ex=[("ps", bank)])

            def stageA(n):
                c, tg = n // 4, n % 4
                s, wv = wv_of(c)
                p2, p3 = n % 2, n % 4
                tsl = slice(tg * TG, (tg + 1) * TG)
                bank = cnt["pj"] % 4
                cnt["pj"] += 1
                for dc in range(NDC):
                    S.add("pe", lambda e, dc=dc: e.matmul(ps[:, bank, :], lhsT=wv[:, dc, 0:128], rhs=hn[:, dc, tsl], start=(dc == 0), stop=(dc == NDC - 1)),
                          reads=[("w", s), ("hn", dc, tg)], ex=[("ps", bank)])
                if tg == 3:
                    wdone()
                S.add("dve", lambda e: e.tensor_scalar_add(out=xh[p2][:, 3:515], in0=ps[:, bank, :], scalar1=smc(("b_in", l), c)),
                      reads=["wsm", *AE], writes=[("xh", p2)], ex=[("ps", bank)])
                if tg == 0:
                    S.add("pool", lambda e: e.memset(xh[p2][:, 0:3], 0.0), reads=[*AE], writes=[("xhalo", p2)])
                else:
                    S.add("pool", lambda e: e.tensor_copy(out=xh[p2][:, 0:3], in_=xh[1 - p2][:, 512:515]),
                          reads=[("xh", 1 - p2), *AE], writes=[("xhalo", p2)])
                S.add("pool", lambda e: e.tensor_scalar(out=xc[p3][:], in0=xh[p2][:, 0:512], scalar1=smc(("conv_w", l), c), scalar2=smc(("conv_b", l), c), op0=ALU.mult, op1=ALU.add),
                      reads=[("xh", p2), ("xhalo", p2), "wsm", *AE], writes=[("xc", p3)])

            def stageAd(n):
                c, tg = n // 4, n % 4
                p2, p3 = n % 2, n % 4
                for tap in range(1, 4):
                    S.add("dve", lambda e, tap=tap: e.scalar_tensor_tensor(out=xc[p3][:], in0=xh[p2][:, tap:tap + 512], scalar=smc(("conv_w", l), tap * 8 + c), in1=xc[p3][:], op0=ALU.mult, op1=ALU.add),
                          reads=[("xh", p2), ("xhalo", p2), ("xc", p3), "wsm", *AE], writes=[("xc", p3)])
                S.add("dve", lambda e: e.tensor_copy(out=xcb[p2][:], in_=xc[p3][:]), reads=[("xc", p3), *AE], writes=[("xcb", p2)])

            def stageA2(n):
                c, tg = n // 4, n % 4
                p2 = n % 2
                bank_r = 4 + 2 * p2
                bank_i = bank_r + 1
                S.add("pe", lambda e: e.matmul(ps[:, bank_r, :], lhsT=wg[:, 0, c, :], rhs=xcb[p2][:], start=True, stop=True),
                      reads=["gate", ("xcb", p2), *AE], ex=[("ps", bank_r)])
                S.add("pe", lambda e: e.matmul(ps[:, bank_i, :], lhsT=wg[:, 1, c, :], rhs=xcb[p2][:], start=True, stop=True),
                      reads=["gate", ("xcb", p2), *AE], ex=[("ps", bank_i)])

            def stageB(n):
                c, tg = n // 4, n % 4
                p2, p3 = n % 2, n % 3
                bank_r = 4 + 2 * p2
                bank_i = bank_r + 1
                S.add("act", lambda e: e.activation(out=ab[p2][:], in_=ps[:, bank_r, :], func=AF.Tanh, scale=0.5, bias=rder[:, 16 + c:17 + c]),
                      reads=["rder", *AE], writes=[("ab", p2)], ex=[("ps", bank_r)])
                S.add("act", lambda e: e.activation(out=tb[p3][:], in_=ps[:, bank_i, :], func=AF.Tanh, scale=0.5, bias=rder[:, 24 + c:25 + c]),
                      reads=["rder", *AE], writes=[("tb", p3)], ex=[("ps", bank_i)])
                S.add("act", lambda e: e.activation(out=mb[p2][:], in_=ab[p2][:], func=AF.Exp, scale=rder[:, c:c + 1], bias=rder[:, c:c + 1]),
                      reads=[("ab", p2), "rder", *AE], writes=[("mb", p2)])
                S.add("act", lambda e: e.activation(out=ab[p2][:], in_=ab[p2][:], func=AF.Exp, scale=rder[:, 8 + c:9 + c], bias=rder[:, 8 + c:9 + c]),
                      reads=[("ab", p2), "rder", *AE], writes=[("ab", p2)])
                S.add("act", lambda e: e.activation(out=mb[p2][:], in_=mb[p2][:], func=AF.Ln, scale=-1.0, bias=1.0),
                      reads=[("mb", p2), *AE], writes=[("mb", p2)])
                S.add("act", lambda e: e.activation(out=mb[p2][:], in_=mb[p2][:], func=AF.Exp, scale=0.5),
                      reads=[("mb", p2), *AE], writes=[("mb", p2)])

            def stageC(n):
                c, tg = n // 4, n % 4
                p2, p3 = n % 2, n % 3
                p4 = n % 4
                q3 = (n - 1) % 3
                tsl = slice(tg * TG, (tg + 1) * TG)
                S.add("dve", lambda e: e.scalar_tensor_tensor(out=tb[p3][:], in0=tb[p3][:], scalar=1.0, in1=xc[p4][:], op0=ALU.add, op1=ALU.mult),
                      reads=[("tb", p3), ("xc", p4), *AE], writes=[("tb", p3)])
                S.add("dve", lambda e: e.scalar_tensor_tensor(out=mb[p2][:], in0=tb[p3][:], scalar=0.5, in1=mb[p2][:], op0=ALU.mult, op1=ALU.mult),
                      reads=[("tb", p3), ("mb", p2), *AE], writes=[("mb", p2)])
                if tg == 0:
                    S.add("dve", lambda e: e.tensor_tensor_scan(out=tb[p3][:], data0=ab[p2][:], data1=mb[p2][:], initial=0.0, op0=ALU.mult, op1=ALU.add),
                          reads=[("ab", p2), ("mb", p2), ("tb", p3), *AE], writes=[("tb", p3)])
                else:
                    S.add("dve", lambda e: e.tensor_tensor_scan(out=tb[p3][:], data0=ab[p2][:], data1=mb[p2][:], initial=tb[q3][:, 511:512], op0=ALU.mult, op1=ALU.add),
                          reads=[("ab", p2), ("mb", p2), ("tb", p3), ("tb", q3), *AE], writes=[("tb", p3)])
                S.add("pool", lambda e: e.tensor_tensor(out=oT[:, c, tsl], in0=tb[p3][:], in1=yb[c % 2][:, tsl], op=ALU.mult),
                      reads=[("tb", p3), ("yb", c % 2, tg), "AE_C", *AE], writes=[("oT", c, tg)])

            NIT = 32
            stageY(0)
            stageA(0)
            stageAd(0)
            stageA(1)
            stageAd(1)
            stageA2(0)
            stageA(2)
            stageAd(2)
            stageA2(1)
            stageB(0)
            for n in range(NIT):
                if n + 3 < NIT:
                    if (n + 3) % 4 == 0:
                        stageY((n + 3) // 4)
                    stageA(n + 3)
                if n + 2 < NIT:
                    stageA2(n + 2)
                if n + 1 < NIT:
                    stageB(n + 1)
                stageC(n)
                if n + 3 < NIT:
                    stageAd(n + 3)
            out_proj(oT, ("b_o", l))

        OT_OFF = ARENA_W - 8192
        SPEC_NORM = [("sq", (8, 512), BF16, 6144), ("rstd0", (512,), F32), ("rstd1", (512,), F32),
                     ("stmp0", (512,), F32), ("stmp1", (512,), F32)]
        SPEC_MLP = [("z", (4, 2048), BF16), ("rtmp", (4, 512), F32)]
        SPEC_ATT = [("oT", (8, 2048), BF16, OT_OFF), ("QT0", (2048,), BF16, 0), ("KA0", (2048,), BF16), ("KB0", (2048,), BF16), ("Vx0", (16, 192), BF16),
                    ("QT1", (2048,), BF16), ("KA1", (2048,), BF16), ("KB1", (2048,), BF16), ("Vx1", (16, 192), BF16),
                    ("expS0", (512,), BF16), ("expS1", (512,), BF16), ("expS2", (512,), BF16), ("expS3", (512,), BF16),
                    ("rc0", (512,), F32), ("rc1", (512,), F32), ("relb0", (2, 256), F32),
                    ("expB0", (2, 256), BF16), ("expB1", (2, 256), BF16)]
        SPEC_REC = [("oT", (8, 2048), BF16, OT_OFF), ("yb0", (2048,), BF16, 0), ("yb1", (2048,), BF16), ("xh0", (516,), F32), ("xh1", (516,), F32),
                    ("xc0", (512,), F32), ("xc1", (512,), F32), ("xc2", (512,), F32), ("xc3", (512,), F32), ("xcb0", (512,), BF16), ("xcb1", (512,), BF16),
                    ("ab0", (512,), F32), ("ab1", (512,), F32), ("mb0", (512,), F32), ("mb1", (512,), F32),
                    ("tb0", (512,), F32), ("tb1", (512,), F32), ("tb2", (512,), F32)]

        for l in layers:
            if not DEBUG.get("skip_mix"):
                carve(SPEC_NORM)
                rmsnorm(("gmix", l))
                if l % 2 == 0:
                    carve(SPEC_ATT)
                    attention(l)
                else:
                    carve(SPEC_REC)
                    recurrent(l)
            else:
                for _ in range(10):
                    wnext(); wdone()
            if not DEBUG.get("skip_mlp"):
                carve(SPEC_NORM)
                rmsnorm(("gmlp", l))
                carve(SPEC_MLP)
                mlp()
            else:
                for _ in range(16):
                    wnext(); wdone()
        if final:
            carve(SPEC_NORM)
            rmsnorm("gfin", to_h=True)
        for tg in range(NTG):
            for hf in range(2):
                S.add("sp", lambda e, tg=tg, hf=hf: e.dma_start(out=outT_v[:, hf * 4:(hf + 1) * 4, tg * TG:(tg + 1) * TG], in_=h[:, hf * 4:(hf + 1) * 4, tg * TG:(tg + 1) * TG]),
                      reads=[("h", dc, tg) for dc in range(hf * 4, (hf + 1) * 4)], chan="hout%d" % (tg * 2 + hf))
        assert state["next_use"] == n_pieces, (state["next_use"], n_pieces)
        S.finalize(st)
        with nc.Block() as block:
            S.emit(block)
    return nc


_CACHE = {}


def run_layers(inp, x, layers, final, n_cores=None):
    B = x.shape[0]
    wbig, wsm, relb, wgate, sm_off = prep_weights(inp, layers, final)
    nc = build_program(layers, final, wbig.shape[0], wsm.shape[1], relb.shape[0], wgate.shape[0], sm_off)
    in_maps = []
    for b in range(B):
        in_maps.append({"xT": np.ascontiguousarray(x[b].T), "wbig": wbig, "wsm": wsm, "relb": relb, "wgate": wgate})
    res = run_bass_kernel_spmd(nc, in_maps, core_ids=list(range(B)))
    out = np.stack([np.ascontiguousarray(r["outT"].T) for r in res.results], axis=0)
    return out


def kernel(**inputs):
    inp = {k: np.asarray(v, dtype=np.float32) for k, v in inputs.items()}
    x = inp["x"]
    out = run_layers(inp, x, [0, 1, 2, 3], True)
    return out.astype(np.float32)
```

```python
from contextlib import ExitStack
import numpy as np
import concourse.bass as bass
import concourse.mybir as mybir
from concourse.bass_utils import run_bass_kernel_spmd

F32 = mybir.dt.float32
BF16 = mybir.dt.bfloat16
AF = mybir.ActivationFunctionType
ALU = mybir.AluOpType

D = 1024
SEQ = 2048
NTG = 4
TG = 512
NDC = 8
NSLOT = 3
SLOT_N = 4096
NEG = -30000.0
EPS = 1e-6
DEBUG = {}


class _Op:
    __slots__ = ("idx", "eng", "fn", "deps", "chan", "sig", "waits", "has_dep")


class Sched:
    def __init__(self, nc, same_engine_sync=True):
        self.nc = nc
        self.ops = []
        self.last_writer = {}
        self.readers = {}
        self.same_engine_sync = same_engine_sync

    def add(self, eng, fn, reads=(), writes=(), ex=(), chan=None):
        op = _Op()
        op.idx = len(self.ops)
        op.eng = eng
        op.fn = fn
        op.chan = chan
        op.sig = None
        op.has_dep = False
        reads = list(reads)
        writes = list(writes) + list(ex)
        deps = set()
        for k in reads:
            w = self.last_writer.get(k)
            if w is not None:
                deps.add(w)
        for k in writes:
            w = self.last_writer.get(k)
            if w is not None:
                deps.add(w)
            rd = self.readers.get(k)
            if rd:
                for v in rd.values():
                    if isinstance(v, list):
                        deps.update(v)
                    else:
                        deps.add(v)
        for k in reads:
            rd = self.readers.setdefault(k, {})
            if chan is not None:
                rd.setdefault(("dma", eng), []).append(op.idx)
            else:
                rd[eng] = op.idx
        for k in writes:
            self.last_writer[k] = op.idx
            self.readers[k] = {}
        deps.discard(op.idx)
        op.deps = deps
        self.ops.append(op)
        return op

    def finalize(self, stack):
        nc = self.nc
        ops = self.ops
        for op in ops:
            keep = set()
            for d in op.deps:
                dop = ops[d]
                if dop.chan is None and op.chan is None and dop.eng == op.eng:
                    if dop.eng == "pe" or not self.same_engine_sync:
                        continue
                keep.add(d)
            op.deps = keep
            for d in keep:
                ops[d].has_dep = True
        sems = {}
        counts = {}

        def get_sem(name):
            if name not in sems:
                sems[name] = stack.enter_context(nc.semaphore("s_" + name))
                counts[name] = 0

        for op in ops:
            if op.chan is not None:
                nm = "c_" + op.chan
                get_sem(nm)
                counts[nm] += 16
                op.sig = (nm, counts[nm])
            elif op.has_dep:
                nm = "e_" + op.eng
                get_sem(nm)
                counts[nm] += 1
                op.sig = (nm, counts[nm])
        waited = {}
        for op in ops:
            need = {}
            for d in op.deps:
                s, v = ops[d].sig
                if v > need.get(s, 0):
                    need[s] = v
            w = waited.setdefault(op.eng, {})
            op.waits = []
            for s, v in need.items():
                if w.get(s, 0) < v:
                    w[s] = v
                    op.waits.append((s, v))
        self.sems = sems
        self.counts = counts
        per_eng = {}
        for op in ops:
            per_eng.setdefault(op.eng, []).append(op)
        self.per_eng = per_eng

    def emit(self, block):
        sems = self.sems
        per_eng = self.per_eng
        counts = self.counts

        def run(engname, e):
            for op in per_eng.get(engname, []):
                for s, v in op.waits:
                    e.wait_ge(sems[s], v)
                ins = op.fn(e)
                if op.sig is not None:
                    ins.then_inc(sems[op.sig[0]], 16 if op.chan is not None else 1)
            if engname == "sp":
                for s, v in counts.items():
                    if v > 0:
                        e.wait_ge(sems[s], v)

        @block.sync
        def _(e):
            run("sp", e)

        @block.tensor
        def _(e):
            run("pe", e)

        @block.scalar
        def _(e):
            run("act", e)

        @block.vector
        def _(e):
            run("dve", e)

        @block.gpsimd
        def _(e):
            run("pool", e)


def _piece(w_sub):
    k, cols = w_sub.shape
    nch = k // 128
    out = np.ascontiguousarray(w_sub.reshape(nch, 128, cols).transpose(1, 0, 2)).reshape(128, nch * cols)
    if out.shape[1] < SLOT_N:
        pad = np.zeros((128, SLOT_N - out.shape[1]), np.float32)
        out = np.concatenate([out, pad], axis=1)
    return out


def _col(v):
    return np.ascontiguousarray(v.reshape(-1, 128).T)


def layer_kinds(layers):
    return ["attn" if l % 2 == 0 else "rec" for l in layers]


def prep_weights(inp, layers, final):
    pieces = []
    sm = []
    sm_off = {}

    def add_sm(name, arr):
        sm_off[name] = sum(a.shape[1] for a in sm)
        sm.append(np.ascontiguousarray(arr, dtype=np.float32))

    relb = []
    gates = []
    maskc = np.zeros((128, 256), np.float32)
    maskc[64:128, 0:64] = NEG
    add_sm("maskc", maskc)
    kq = np.arange(128)
    rel0 = kq[None, :] - kq[:, None]
    idx_d0 = np.clip(rel0, -128, 128) + 128
    idx_d1 = np.clip(rel0 + 128, -128, 128) + 128
    for l in layers:
        j = l // 2
        add_sm(("gmix", l), _col(inp["norm_mix"][l]))
        add_sm(("gmlp", l), _col(inp["norm_mlp"][l]))
        if l % 2 == 0:
            wqkv = inp["attn_w_qkv"][j]
            for p in range(8):
                sub = np.concatenate([wqkv[:, p * 128:(p + 1) * 128],
                                      wqkv[:, 1024 + p * 128:1024 + (p + 1) * 128],
                                      wqkv[:, 2048 + p * 128:2048 + (p + 1) * 128]], axis=1)
                pieces.append(_piece(sub))
                tab = inp["attn_rel_bias"][j]
                rb = np.zeros((128, 512), np.float32)
                for hh in range(2):
                    t = tab[2 * p + hh]
                    rb[:, hh * 256:hh * 256 + 128] = t[idx_d0]
                    rb[:, hh * 256 + 128:hh * 256 + 256] = t[idx_d1]
                relb.append(rb)
            wo = inp["attn_w_o"][j]
            for hf in range(2):
                pieces.append(_piece(wo[:, hf * 512:(hf + 1) * 512]))
            add_sm(("chcol", l), np.broadcast_to(inp["attn_rel_bias"][j][:, 256].reshape(1, 16), (128, 16)))
        else:
            ga = inp["rec_w_ga"][j]
            gx = inp["rec_w_gx"][j]
            gp = np.concatenate([ga.transpose(1, 0, 2).reshape(128, 1024), gx.transpose(1, 0, 2).reshape(128, 1024)], axis=1)
            gates.append(np.ascontiguousarray(gp))
            win = inp["rec_w_in"][j]
            for c in range(8):
                sub = np.concatenate([win[:, c * 128:(c + 1) * 128], win[:, 1024 + c * 128:1024 + (c + 1) * 128]], axis=1)
                pieces.append(_piece(sub))
            wo = inp["rec_w_o"][j]
            for hf in range(2):
                pieces.append(_piece(wo[:, hf * 512:(hf + 1) * 512]))
            add_sm(("b_in", l), _col(inp["rec_b_in"][j]))
            add_sm(("conv_w", l), np.concatenate([_col(inp["rec_conv_w"][j][t]) for t in range(4)], axis=1))
            add_sm(("conv_b", l), _col(inp["rec_conv_b"][j]))
            add_sm(("b_ga", l), _col(inp["rec_b_ga"][j].reshape(-1)))
            add_sm(("b_gx", l), _col(inp["rec_b_gx"][j].reshape(-1)))
            add_sm(("a_param", l), _col(inp["rec_a_param"][j]))
            add_sm(("b_o", l), _col(inp["rec_b_o"][j]))
        w1 = inp["mlp_w1"][l]
        w2 = inp["mlp_w2"][l]
        for g in range(8):
            pieces.append(_piece(w1[:, g * 512:(g + 1) * 512]))
            pieces.append(_piece(w2[g * 512:(g + 1) * 512, :]))
    if final:
        add_sm("gfin", _col(inp["norm_final"]))
    wbig = np.ascontiguousarray(np.stack(pieces, axis=0))
    wsm = np.ascontiguousarray(np.concatenate(sm, axis=1))
    if not relb:
        relb = [np.zeros((128, 512), np.float32)]
    relb = np.ascontiguousarray(np.stack(relb, axis=0))
    if not gates:
        gates = [np.zeros((128, 2048), np.float32)]
    wgate = np.ascontiguousarray(np.stack(gates, axis=0))
    return wbig, wsm, relb, wgate, sm_off


def build_program(layers, final, n_pieces, n_sm, n_relb, n_gate, sm_off):
    nc = bass.Bass("TRN2", target_bir_lowering=False)
    xT = nc.dram_tensor("xT", [D, SEQ], F32, kind="ExternalInput").ap()
    wbig = nc.dram_tensor("wbig", [n_pieces, 128, SLOT_N], F32, kind="ExternalInput").ap()
    wsm_d = nc.dram_tensor("wsm", [128, n_sm], F32, kind="ExternalInput").ap()
    relb_d = nc.dram_tensor("relb", [n_relb, 128, 512], F32, kind="ExternalInput").ap()
    wgate_d = nc.dram_tensor("wgate", [n_gate, 128, 2048], F32, kind="ExternalInput").ap()
    outT = nc.dram_tensor("outT", [D, SEQ], F32, kind="ExternalOutput").ap()
    xT_v = xT.rearrange("(c p) t -> p c t", p=128)
    outT_v = outT.rearrange("(c p) t -> p c t", p=128)

    with ExitStack() as st:
        def sb(name, shape, dt):
            return st.enter_context(nc.sbuf_tensor("k_" + name, shape, dt))

        h = sb("h", [128, NDC, SEQ], F32)
        hn = sb("hn", [128, NDC, SEQ], BF16)
        wsl = sb("wsl", [128, NSLOT, SLOT_N], BF16)
        wsm = sb("wsm_sb", [128, n_sm], F32)
        ones = sb("ones", [128, 128], BF16)
        dummy = sb("dummy", [128, 8], F32)
        rder = sb("rder", [128, 40], F32)
        gatebuf = sb("gatebuf", [128, 2048], BF16)
        ARENA_W = 20480
        arena = sb("arena", [128, ARENA_W], F32)
        ps = st.enter_context(nc.psum_tensor("k_ps", [128, 8, 512], F32))
        arena_t = {}

        def carve(spec):
            arena_t.clear()
            off = 0
            for ent in spec:
                name, shape, dt = ent[0], ent[1], ent[2]
                if len(ent) > 3:
                    off = ent[3]
                n = int(np.prod(shape))
                words = n // 2 if dt == BF16 else n
                v = arena[:, off:off + words]
                if dt == BF16:
                    v = v.bitcast(BF16)
                if len(shape) == 2:
                    v = v.rearrange("p (a b) -> p a b", b=shape[1])
                elif len(shape) == 3:
                    v = v.rearrange("p (a b c) -> p a b c", b=shape[1], c=shape[2])
                arena_t[name] = v
                off += words
            assert off <= ARENA_W, (off, ARENA_W)

        S = Sched(nc)
        AE = ["AE_A", "AE_B", "AE_C"]

        def arena_join(keys):
            S.add("pool", lambda e: e.memset(dummy[:, 0:1], 0.0), writes=list(keys) + ["dummy"])

        state = {"next_load": 0, "next_use": 0}

        def wload():
            k = state["next_load"]
            if k >= n_pieces:
                return
            state["next_load"] += 1
            slot = k % NSLOT
            src = wbig[k].rearrange("p (a b) -> p a b", b=1024)
            dst = wsl[:, slot, :].rearrange("p (a b) -> p a b", b=1024)
            S.add("pool", lambda e: e.dma_start(out=dst, in_=src), writes=[("w", slot)], chan="w%d" % slot)

        def wnext():
            k = state["next_use"]
            state["next_use"] += 1
            assert k < state["next_load"], "piece used before load emitted"
            return k % NSLOT

        def wdone():
            wload()

        def smc(name, i=0, n=1):
            o = sm_off[name] + i
            return wsm[:, o:o + n]

        S.add("sp", lambda e: e.dma_start(out=wsm[:], in_=wsm_d[:, :]), writes=["wsm"], chan="wsm")
        for tg in range(NTG):
            for hf in range(2):
                S.add("sp", lambda e, tg=tg, hf=hf: e.dma_start(out=h[:, hf * 4:(hf + 1) * 4, tg * TG:(tg + 1) * TG], in_=xT_v[:, hf * 4:(hf + 1) * 4, tg * TG:(tg + 1) * TG]),
                      writes=[("h", dc, tg) for dc in range(hf * 4, (hf + 1) * 4)], chan="hin%d" % (tg * 2 + hf))
        S.add("dve", lambda e: e.memset(ones[:], 1.0), writes=["ones"])
        for _ in range(NSLOT):
            wload()

        def rmsnorm(gname, to_h=False):
            AE[:] = ["AE_B"]
            arena_join(["AE_B"])
            sq = arena_t["sq"]
            rstd = [arena_t["rstd0"], arena_t["rstd1"]]
            stmp = [arena_t["stmp0"], arena_t["stmp1"]]
            for tg in range(NTG):
                par = tg % 2
                tsl = slice(tg * TG, (tg + 1) * TG)
                bank = par
                for dc in range(NDC):
                    S.add("act", lambda e, dc=dc, tsl=tsl: e.activation(out=sq[:, dc, :], in_=h[:, dc, tsl], func=AF.Square),
                          reads=[("h", dc, tg), *AE], writes=[("sq", dc)])
                for dc in range(NDC):
                    S.add("pe", lambda e, dc=dc, bank=bank: e.matmul(ps[:, bank, :], lhsT=ones[:], rhs=sq[:, dc, :], start=(dc == 0), stop=(dc == NDC - 1)),
                          reads=[("sq", dc), "ones", *AE], ex=[("ps", bank)])
                if to_h:
                    S.add("act", lambda e, par=par, bank=bank: e.activation(out=stmp[par][:], in_=ps[:, bank, :], func=AF.Sqrt, scale=1.0 / D, bias=EPS),
                          reads=[*AE], writes=[("stmp", par)], ex=[("ps", bank)])
                    S.add("dve", lambda e, par=par: e.reciprocal(out=rstd[par][:], in_=stmp[par][:]),
                          reads=[("stmp", par), *AE], writes=[("rstd", par)])
                else:
                    S.add("act", lambda e, par=par, bank=bank: e.activation(out=stmp[par][:], in_=ps[:, bank, :], func=AF.Ln, scale=1.0 / D, bias=EPS),
                          reads=[*AE], writes=[("stmp", par)], ex=[("ps", bank)])
                    S.add("act", lambda e, par=par: e.activation(out=rstd[par][:], in_=stmp[par][:], func=AF.Exp, scale=-0.5),
                          reads=[("stmp", par), *AE], writes=[("rstd", par)])
                for dc in range(NDC):
                    if to_h:
                        S.add("dve", lambda e, dc=dc, par=par, tsl=tsl: e.scalar_tensor_tensor(out=h[:, dc, tsl], in0=h[:, dc, tsl], scalar=smc(gname, dc), in1=rstd[par][:], op0=ALU.mult, op1=ALU.mult),
                              reads=[("h", dc, tg), ("rstd", par), "wsm", *AE], writes=[("h", dc, tg)])
                    else:
                        S.add("dve", lambda e, dc=dc, par=par, tsl=tsl: e.scalar_tensor_tensor(out=hn[:, dc, tsl], in0=h[:, dc, tsl], scalar=smc(gname, dc), in1=rstd[par][:], op0=ALU.mult, op1=ALU.mult),
                              reads=[("h", dc, tg), ("rstd", par), "wsm", *AE], writes=[("hn", dc, tg)])

        def mlp():
            AE[:] = ["AE_A"]
            arena_join(["AE_A"])
            z = arena_t["z"]
            rtmp = arena_t["rtmp"]
            cnt = {"a": 0, "b": 0}
            for g in range(8):
                s1 = wnext()
                w1v = wsl[:, s1, :].rearrange("p (c f) -> p c f", f=512)
                for tg in range(NTG):
                    tsl = slice(tg * TG, (tg + 1) * TG)
                    for fc in range(4):
                        bank = cnt["a"] % 4
                        rt = cnt["a"] % 4
                        cnt["a"] += 1
                        for dc in range(NDC):
                            S.add("pe", lambda e, dc=dc, fc=fc, bank=bank, tsl=tsl, w1v=w1v: e.matmul(ps[:, bank, :], lhsT=w1v[:, dc, fc * 128:(fc + 1) * 128], rhs=hn[:, dc, tsl], start=(dc == 0), stop=(dc == NDC - 1)),
                                  reads=[("w", s1), ("hn", dc, tg)], ex=[("ps", bank)])
                        S.add("act", lambda e, bank=bank, rt=rt: e.activation(out=rtmp[:, rt, :], in_=ps[:, bank, :], func=AF.Relu),
                              reads=[*AE], writes=[("rtmp", rt)], ex=[("ps", bank)])
                        S.add("act", lambda e, rt=rt, fc=fc, tsl=tsl: e.activation(out=z[:, fc, tsl], in_=rtmp[:, rt, :], func=AF.Square),
                              reads=[("rtmp", rt), *AE], writes=[("z", fc, tg)])
                wdone()
                s2 = wnext()
                w2v = wsl[:, s2, :].rearrange("p (c f) -> p c f", f=1024)
                for tg in range(NTG):
                    tsl = slice(tg * TG, (tg + 1) * TG)
                    for dcs in range(NDC):
                        bank = 4 + cnt["b"] % 4
                        cnt["b"] += 1
                        for fc in range(4):
                            S.add("pe", lambda e, fc=fc, dcs=dcs, bank=bank, tsl=tsl, w2v=w2v: e.matmul(ps[:, bank, :], lhsT=w2v[:, fc, dcs * 128:(dcs + 1) * 128], rhs=z[:, fc, tsl], start=(fc == 0), stop=(fc == 3)),
                                  reads=[("w", s2), ("z", fc, tg), *AE], ex=[("ps", bank)])
                        S.add("dve", lambda e, dcs=dcs, bank=bank, tsl=tsl: e.tensor_tensor(out=h[:, dcs, tsl], in0=ps[:, bank, :], in1=h[:, dcs, tsl], op=ALU.add),
                              reads=[("h", dcs, tg)], writes=[("h", dcs, tg)], ex=[("ps", bank)])
                wdone()

        def out_proj(oT, bias_name=None):
            AE[:] = ["AE_C"]
            cnt = 0
            ss = [wnext(), wnext()]
            wvs = [wsl[:, sx, :].rearrange("p (c f) -> p c f", f=512) for sx in ss]
            for tg in range(NTG):
                tsl = slice(tg * TG, (tg + 1) * TG)
                for hf in range(2):
                    s_, wv = ss[hf], wvs[hf]
                    for dcs in range(4):
                        dco = hf * 4 + dcs
                        bank = 6 + cnt % 2
                        cnt += 1
                        for pc in range(NDC):
                            S.add("pe", lambda e, pc=pc, dcs=dcs, bank=bank, tsl=tsl, wv=wv: e.matmul(ps[:, bank, :], lhsT=wv[:, pc, dcs * 128:(dcs + 1) * 128], rhs=oT[:, pc, tsl], start=(pc == 0), stop=(pc == NDC - 1)),
                                  reads=[("w", s_), ("oT", pc, tg), *AE], ex=[("ps", bank)])
                        if bias_name is None:
                            S.add("dve", lambda e, dco=dco, bank=bank, tsl=tsl: e.tensor_tensor(out=h[:, dco, tsl], in0=ps[:, bank, :], in1=h[:, dco, tsl], op=ALU.add),
                                  reads=[("h", dco, tg)], writes=[("h", dco, tg)], ex=[("ps", bank)])
                        else:
                            S.add("dve", lambda e, dco=dco, bank=bank, tsl=tsl: e.scalar_tensor_tensor(out=h[:, dco, tsl], in0=ps[:, bank, :], scalar=smc(bias_name, dco), in1=h[:, dco, tsl], op0=ALU.add, op1=ALU.add),
                                  reads=[("h", dco, tg), "wsm"], writes=[("h", dco, tg)], ex=[("ps", bank)])
            wdone()
            wdone()

        relb_state = {"n": 0}

        def attention(l):
            AE[:] = ["AE_A", "AE_B"]
            arena_join(["AE_A", "AE_B", "AE_C"])
            oT = arena_t["oT"]
            QT = [arena_t["QT0"], arena_t["QT1"]]
            KAB = [[arena_t["KA0"], arena_t["KB0"]], [arena_t["KA1"], arena_t["KB1"]]]
            expB = [arena_t["expB0"], arena_t["expB1"]]
            Vx = [arena_t["Vx0"], arena_t["Vx1"]]
            expS = [arena_t["expS%d" % i] for i in range(4)]
            rc = [arena_t["rc0"], arena_t["rc1"]]
            rbuf = [arena_t["relb0"], arena_t["relb0"]]
            Bm = arena_t["relb0"]
            for b in range(2):
                S.add("pool", lambda e, b=b: e.memset(Vx[b][:, :, 64:128], 1.0), reads=[*AE], writes=[("Vx_ones", b)])
                S.add("pool", lambda e, b=b: e.memset(KAB[b][1][0:64, :], 0.0), reads=[*AE], writes=[("Kz", b)])
            cnts = {"ss0": 0, "ss1": 0, "po": 0, "pj": 0}
            PJ_BANKS = [6, 7]

            def load_relb(p):
                gi = relb_state["n"]
                relb_state["n"] += 1
                S.add("sp", lambda e, gi=gi, p=p: e.dma_start(out=rbuf[p % 2][:].rearrange("p a b -> p (a b)"), in_=relb_d[gi]), reads=[*AE], writes=[("relb", 0)], chan="relb0")

            def proj_units(p, banks):
                pb = p % 2
                s = wnext()
                wv = wsl[:, s, 0:8 * 384].rearrange("p (c f) -> p c f", f=384)
                units = []
                for which in (0, 1):
                    for tg in range(NTG):
                        ubank = {}

                        def ua(which=which, tg=tg, ubank=ubank):
                            tsl = slice(tg * TG, (tg + 1) * TG)
                            bank = banks[cnts["pj"] % len(banks)]
                            cnts["pj"] += 1
                            ubank[0] = bank
                            for dc in range(NDC // 2):
                                S.add("pe", lambda e, dc=dc: e.matmul(ps[:, bank, :], lhsT=wv[:, dc, which * 128:(which + 1) * 128], rhs=hn[:, dc, tsl], start=(dc == 0), stop=False),
                                      reads=[("w", s), ("hn", dc, tg)], ex=[("ps", bank)])
                        units.append(ua)

                        def u(which=which, tg=tg, ubank=ubank):
                            tsl = slice(tg * TG, (tg + 1) * TG)
                            bank = ubank[0]
                            for dc in range(NDC // 2, NDC):
                                S.add("pe", lambda e, dc=dc: e.matmul(ps[:, bank, :], lhsT=wv[:, dc, which * 128:(which + 1) * 128], rhs=hn[:, dc, tsl], start=False, stop=(dc == NDC - 1)),
                                      reads=[("w", s), ("hn", dc, tg)], ex=[("ps", bank)])
                            if which == 0:
                                S.add("dve", lambda e: e.tensor_scalar_mul(out=QT[pb][:, tsl], in0=ps[:, bank, :], scalar1=0.125),
                                      reads=[*AE], writes=[("qk", pb, 0, tg)], ex=[("ps", bank)])
                            else:
                                S.add("dve", lambda e: e.tensor_copy(out=KAB[pb][0][:, tsl], in_=ps[:, bank, :]),
                                      reads=[*AE], writes=[("qk", pb, 1, tg)], ex=[("ps", bank)])
                                S.add("pool", lambda e: e.tensor_copy(out=KAB[pb][1][64:128, tsl], in_=KAB[pb][0][64:128, tsl]),
                                      reads=[("qk", pb, 1, tg), *AE], writes=[("qk", pb, 2, tg)])
                                S.add("pool", lambda e: e.memset(KAB[pb][0][64:128, tsl], 0.0),
                                      reads=[*AE], writes=[("qk", pb, 1, tg)])
                        units.append(u)
                vbank = {}
                for t in range(16):
                    def u(t=t):
                        tq, t4 = t // 4, t % 4
                        if t4 == 0:
                            vbank[tq] = banks[cnts["pj"] % len(banks)]
                            cnts["pj"] += 1
                        bank = vbank[tq]
                        for dc in range(NDC):
                            S.add("pe", lambda e, dc=dc: e.matmul(ps[:, bank, t4 * 128:(t4 + 1) * 128], lhsT=hn[:, dc, t * 128:(t + 1) * 128], rhs=wv[:, dc, 256:384], start=(dc == 0), stop=(dc == NDC - 1)),
                                  reads=[("w", s), ("hn", dc, t // 4)], ex=[("ps", bank)])
                        if t4 == 3:
                            psv = ps[:, bank, :].rearrange("p (t c) -> p t c", c=128)
                            S.add("dve", lambda e: e.tensor_copy(out=Vx[pb][:, tq * 4:(tq + 1) * 4, 0:64], in_=psv[:, :, 0:64]),
                                  reads=[*AE], writes=[("V", pb, tq, 0)], ex=[("ps", bank)])
                            S.add("dve", lambda e: e.tensor_copy(out=Vx[pb][:, tq * 4:(tq + 1) * 4, 128:192], in_=psv[:, :, 64:128]),
                                  reads=[*AE], writes=[("V", pb, tq, 1)], ex=[("ps", bank)])
                    units.append(u)
                units.append(wdone)
                return units

            load_relb(0)
            for u in proj_units(0, [0, 1, 2, 3, 7]):
                u()
            for p in range(8):
                pb = p % 2
                if p + 1 < 8:
                    units = proj_units(p + 1, PJ_BANKS)
                else:
                    units = []
                for hh in range(2):
                    S.add("dve", lambda e, hh=hh, p=p: e.scalar_tensor_tensor(out=Bm[:, hh, :], in0=Bm[:, hh, :], scalar=smc(("chcol", l), 2 * p + hh), in1=smc("maskc", 0, 256), op0=ALU.subtract, op1=ALU.add),
                          reads=["wsm", *AE], writes=[("relb", 0)])
                    S.add("act", lambda e, hh=hh, p=p: e.activation(out=expB[p % 2][:, hh, :], in_=Bm[:, hh, :], func=AF.Exp),
                          reads=[("relb", 0), *AE], writes=[("expB", p % 2, hh)])
                if p + 1 < 8:
                    load_relb(p + 1)
                items = []
                for G in range(4):
                    for t in range(max(0, 4 * G - 4), 4 * G + 4):
                        for hh in range(2):
                            items.append((G, t, hh))
                nitems = len(items)
                ntot = max(len(units), 1)
                sset_of = {}

                def geom(G, t):
                    j0 = max(t, 4 * G)
                    j1 = min(t + 4, 4 * G + 3)
                    return j0, j1, (j1 - j0 + 1) * 128

                def scores(item, p=p, pb=pb):
                    G, t, hh = item
                    r0 = hh * 64
                    j0, j1, N = geom(G, t)
                    k = cnts["ss0"]
                    cnts["ss0"] += 1
                    es = expS[k % 4]
                    ek = k % 4
                    sset_of[item] = (es, ek)
                    bank = k % 2
                    S.add("pe", lambda e: e.matmul(ps[:, bank, 0:N], lhsT=KAB[pb][hh][:, t * 128:(t + 1) * 128], rhs=QT[pb][:, j0 * 128:(j1 + 1) * 128], start=True, stop=True),
                          reads=[("qk", pb, 0, G), ("qk", pb, 1, t // 4), ("qk", pb, 2, t // 4), ("Kz", pb), *AE], ex=[("ps", bank)])
                    S.add("act", lambda e: e.activation(out=es[:, 0:N], in_=ps[:, bank, 0:N], func=AF.Exp),
                          reads=[*AE], writes=[("expS", ek, 0), ("expS", ek, 1)], ex=[("ps", bank)])
                    if j1 == t + 4:
                        S.add("dve", lambda e: e.memset(es[0:64, N - 64:N], 0.0), reads=[*AE], writes=[("expS", ek, 1)])
                    d0 = j0 - t
                    if d0 == 0:
                        nn = min(2, (j1 - j0 + 1))
                        S.add("dve", lambda e: e.tensor_tensor(out=es[:, 0:nn * 128], in0=es[:, 0:nn * 128], in1=expB[p % 2][:, hh, 0:nn * 128], op=ALU.mult),
                              reads=[("expB", p % 2, hh), *AE], writes=[("expS", ek, 0)])
                    elif d0 == 1:
                        S.add("dve", lambda e: e.tensor_tensor(out=es[:, 0:128], in0=es[:, 0:128], in1=expB[p % 2][:, hh, 128:256], op=ALU.mult),
                              reads=[("expB", p % 2, hh), *AE], writes=[("expS", ek, 0)])

                def pv(item, p=p, pb=pb):
                    G, t, hh = item
                    j0, j1, N = geom(G, t)
                    es, ek = sset_of[item]
                    vcols = slice(0, 128) if hh == 0 else slice(64, 192)
                    so = slice(64, 128) if hh == 0 else slice(0, 64)
                    oo = slice(0, 64) if hh == 0 else slice(64, 128)
                    bank = 2 + 2 * hh + G % 2
                    c0 = (j0 - 4 * G) * 128
                    first = (t == max(0, 4 * G - 4))
                    last = (t == 4 * G + 3)
                    S.add("pe", lambda e: e.matmul(ps[:, bank, c0:c0 + N], lhsT=Vx[pb][:, t, vcols], rhs=es[:, 0:N], start=first, stop=last, skip_group_check=True),
                          reads=[("expS", ek, 0), ("expS", ek, 1), ("V", pb, t // 4, 0), ("V", pb, t // 4, 1), ("Vx_ones", pb), *AE], ex=[("ps", bank)])
                    if last:
                        par = cnts["po"] % 2
                        cnts["po"] += 1
                        tsl = slice(G * TG, (G + 1) * TG)

                        def n1():
                            S.add("act", lambda e: e.activation(out=rc[par][so, :], in_=ps[so, bank, :], func=AF.Ln),
                                  reads=[*AE], writes=[("rc", par)], ex=[("ps", bank)])

                        def n2():
                            S.add("act", lambda e: e.activation(out=rc[par][so, :], in_=rc[par][so, :], func=AF.Exp, scale=-1.0),
                                  reads=[("rc", par), *AE], writes=[("rc", par)])

                        def n3():
                            S.add("dve", lambda e: e.tensor_tensor(out=oT[oo, p, tsl], in0=ps[oo, bank, :], in1=rc[par][so, :], op=ALU.mult),
                                  reads=[("rc", par), "AE_C", *AE], writes=[("oT", p, G)], ex=[("ps", bank)])
                        deferred.append([2, n1])
                        deferred.append([3, n2])
                        deferred.append([4, n3])

                LA = 3
                deferred = []
                for i in range(nitems + LA):
                    if i < nitems:
                        scores(items[i])
                    for d in deferred:
                        d[0] -= 1
                    while deferred and deferred[0][0] <= 0:
                        deferred.pop(0)[1]()
                    if i >= LA:
                        pv(items[i - LA])
                        step = i - LA + 1
                        while units and len(units) * nitems > (nitems - step) * ntot:
                            units.pop(0)()
                while deferred:
                    deferred.pop(0)[1]()
                while units:
                    units.pop(0)()
            out_proj(oT, None)

        gate_state = {"n": 0}

        def recurrent(l):
            AE[:] = ["AE_A", "AE_B"]
            arena_join(["AE_A", "AE_B", "AE_C"])
            oT = arena_t["oT"]
            yb = [arena_t["yb0"], arena_t["yb1"]]
            xh = [arena_t["xh0"], arena_t["xh1"]]
            xc = [arena_t["xc0"], arena_t["xc1"], arena_t["xc2"], arena_t["xc3"]]
            xcb = [arena_t["xcb0"], arena_t["xcb1"]]
            ab = [arena_t["ab0"], arena_t["ab1"]]
            mb = [arena_t["mb0"], arena_t["mb1"]]
            tb = [arena_t["tb0"], arena_t["tb1"], arena_t["tb2"]]
            S.add("act", lambda e: e.activation(out=rder[:, 32:40], in_=smc(("a_param", l), 0, 8), func=AF.Exp, scale=-1.0),
                  reads=["wsm"], writes=["rder_t"])
            S.add("act", lambda e: e.activation(out=rder[:, 32:40], in_=rder[:, 32:40], func=AF.Ln, bias=1.0),
                  reads=["rder_t"], writes=["rder_t"])
            S.add("dve", lambda e: e.tensor_scalar_mul(out=rder[:, 0:8], in0=rder[:, 32:40], scalar1=-8.0), reads=["rder_t"], writes=["rder"])
            S.add("dve", lambda e: e.tensor_scalar_mul(out=rder[:, 8:16], in0=rder[:, 32:40], scalar1=-4.0), reads=["rder_t"], writes=["rder"])
            S.add("dve", lambda e: e.tensor_scalar_mul(out=rder[:, 16:24], in0=smc(("b_ga", l), 0, 8), scalar1=0.5), reads=["wsm"], writes=["rder"])
            S.add("dve", lambda e: e.tensor_scalar_mul(out=rder[:, 24:32], in0=smc(("b_gx", l), 0, 8), scalar1=0.5), reads=["wsm"], writes=["rder"])
            gi = gate_state["n"]
            gate_state["n"] += 1
            S.add("pool", lambda e, gi=gi: e.dma_start(out=gatebuf[:].rearrange("p (a b) -> p a b", b=1024), in_=wgate_d[gi].rearrange("p (a b) -> p a b", b=1024)),
                  writes=["gate"], chan="gate")
            wg = gatebuf[:].rearrange("p (g n e) -> p g n e", g=2, n=8)
            cnt = {"pj": 0}
            wstate = {}

            def wv_of(c):
                if c not in wstate:
                    s = wnext()
                    wstate[c] = (s, wsl[:, s, 0:2048].rearrange("p (c f) -> p c f", f=256))
                return wstate[c]

            def stageY(c):
                s, wv = wv_of(c)
                for tg in range(NTG):
                    tsl = slice(tg * TG, (tg + 1) * TG)
                    bank = cnt["pj"] % 4
                    cnt["pj"] += 1
                    for dc in range(NDC):
                        S.add("pe", lambda e, dc=dc, bank=bank, tsl=tsl: e.matmul(ps[:, bank, :], lhsT=wv[:, dc, 128:256], rhs=hn[:, dc, tsl], start=(dc == 0), stop=(dc == NDC - 1)),
                              reads=[("w", s), ("hn", dc, tg)], ex=[("ps", bank)])
                    S.add("act", lambda e, bank=bank, tsl=tsl: e.activation(out=yb[c % 2][:, tsl], in_=ps[:, bank, :], func=AF.Gelu_apprx_tanh, bias=smc(("b_in", l), 8 + c)),
                          reads=["wsm", *AE], writes=[("yb", c % 2, tg)], ex=[("ps", bank)])

            def stageA(n):
                c, tg = n // 4, n % 4
                s, wv = wv_of(c)
                p2, p3 = n % 2, n % 4
                tsl = slice(tg * TG, (tg + 1) * TG)
                bank = cnt["pj"] % 4
                cnt["pj"] += 1
                for dc in range(NDC):
                    S.add("pe", lambda e, dc=dc: e.matmul(ps[:, bank, :], lhsT=wv[:, dc, 0:128], rhs=hn[:, dc, tsl], start=(dc == 0), stop=(dc == NDC - 1)),
                          reads=[("w", s), ("hn", dc, tg)], ex=[("ps", bank)])
                if tg == 3:
                    wdone()
                S.add("dve", lambda e: e.tensor_scalar_add(out=xh[p2][:, 3:515], in0=ps[:, bank, :], scalar1=smc(("b_in", l), c)),
                      reads=["wsm", *AE], writes=[("xh", p2)], ex=[("ps", bank)])
                if tg == 0:
                    S.add("pool", lambda e: e.memset(xh[p2][:, 0:3], 0.0), reads=[*AE], writes=[("xhalo", p2)])
                else:
                    S.add("pool", lambda e: e.tensor_copy(out=xh[p2][:, 0:3], in_=xh[1 - p2][:, 512:515]),
                          reads=[("xh", 1 - p2), *AE], writes=[("xhalo", p2)])
                S.add("pool", lambda e: e.tensor_scalar(out=xc[p3][:], in0=xh[p2][:, 0:512], scalar1=smc(("conv_w", l), c), scalar2=smc(("conv_b", l), c), op0=ALU.mult, op1=ALU.add),
                      reads=[("xh", p2), ("xhalo", p2), "wsm", *AE], writes=[("xc", p3)])

            def stageAd(n):
                c, tg = n // 4, n % 4
                p2, p3 = n % 2, n % 4
                for tap in range(1, 4):
                    S.add("dve", lambda e, tap=tap: e.scalar_tensor_tensor(out=xc[p3][:], in0=xh[p2][:, tap:tap + 512], scalar=smc(("conv_w", l), tap * 8 + c), in1=xc[p3][:], op0=ALU.mult, op1=ALU.add),
                          reads=[("xh", p2), ("xhalo", p2), ("xc", p3), "wsm", *AE], writes=[("xc", p3)])
                S.add("dve", lambda e: e.tensor_copy(out=xcb[p2][:], in_=xc[p3][:]), reads=[("xc", p3), *AE], writes=[("xcb", p2)])

            def stageA2(n):
                c, tg = n // 4, n % 4
                p2 = n % 2
                bank_r = 4 + 2 * p2
                bank_i = bank_r + 1
                S.add("pe", lambda e: e.matmul(ps[:, bank_r, :], lhsT=wg[:, 0, c, :], rhs=xcb[p2][:], start=True, stop=True),
                      reads=["gate", ("xcb", p2), *AE], ex=[("ps", bank_r)])
                S.add("pe", lambda e: e.matmul(ps[:, bank_i, :], lhsT=wg[:, 1, c, :], rhs=xcb[p2][:], start=True, stop=True),
                      reads=["gate", ("xcb", p2), *AE], ex=[("ps", bank_i)])

            def stageB(n):
                c, tg = n // 4, n % 4
                p2, p3 = n % 2, n % 3
                bank_r = 4 + 2 * p2
                bank_i = bank_r + 1
                S.add("act", lambda e: e.activation(out=ab[p2][:], in_=ps[:, bank_r, :], func=AF.Tanh, scale=0.5, bias=rder[:, 16 + c:17 + c]),
                      reads=["rder", *AE], writes=[("ab", p2)], ex=[("ps", bank_r)])
                S.add("act", lambda e: e.activation(out=tb[p3][:], in_=ps[:, bank_i, :], func=AF.Tanh, scale=0.5, bias=rder[:, 24 + c:25 + c]),
                      reads=["rder", *AE], writes=[("tb", p3)], ex=[("ps", bank_i)])
                S.add("act", lambda e: e.activation(out=mb[p2][:], in_=ab[p2][:], func=AF.Exp, scale=rder[:, c:c + 1], bias=rder[:, c:c + 1]),
                      reads=[("ab", p2), "rder", *AE], writes=[("mb", p2)])
                S.add("act", lambda e: e.activation(out=ab[p2][:], in_=ab[p2][:], func=AF.Exp, scale=rder[:, 8 + c:9 + c], bias=rder[:, 8 + c:9 + c]),
                      reads=[("ab", p2), "rder", *AE], writes=[("ab", p2)])
                S.add("act", lambda e: e.activation(out=mb[p2][:], in_=mb[p2][:], func=AF.Ln, scale=-1.0, bias=1.0),
                      reads=[("mb", p2), *AE], writes=[("mb", p2)])
                S.add("act", lambda e: e.activation(out=mb[p2][:], in_=mb[p2][:], func=AF.Exp, scale=0.5),
                      reads=[("mb", p2), *AE], writes=[("mb", p2)])

            def stageC(n):
                c, tg = n // 4, n % 4
                p2, p3 = n % 2, n % 3
                p4 = n % 4
                q3 = (n - 1) % 3
                tsl = slice(tg * TG, (tg + 1) * TG)
                S.add("dve", lambda e: e.scalar_tensor_tensor(out=tb[p3][:], in0=tb[p3][:], scalar=1.0, in1=xc[p4][:], op0=ALU.add, op1=ALU.mult),
                      reads=[("tb", p3), ("xc", p4), *AE], writes=[("tb", p3)])
                S.add("dve", lambda e: e.scalar_tensor_tensor(out=mb[p2][:], in0=tb[p3][:], scalar=0.5, in1=mb[p2][:], op0=ALU.mult, op1=ALU.mult),
                      reads=[("tb", p3), ("mb", p2), *AE], writes=[("mb", p2)])
                if tg == 0:
                    S.add("dve", lambda e: e.tensor_tensor_scan(out=tb[p3][:], data0=ab[p2][:], data1=mb[p2][:], initial=0.0, op0=ALU.mult, op1=ALU.add),
                          reads=[("ab", p2), ("mb", p2), ("tb", p3), *AE], writes=[("tb", p3)])
                else:
                    S.add("dve", lambda e: e.tensor_tensor_scan(out=tb[p3][:], data0=ab[p2][:], data1=mb[p2][:], initial=tb[q3][:, 511:512], op0=ALU.mult, op1=ALU.add),
                          reads=[("ab", p2), ("mb", p2), ("tb", p3), ("tb", q3), *AE], writes=[("tb", p3)])
                S.add("pool", lambda e: e.tensor_tensor(out=oT[:, c, tsl], in0=tb[p3][:], in1=yb[c % 2][:, tsl], op=ALU.mult),
                      reads=[("tb", p3), ("yb", c % 2, tg), "AE_C", *AE], writes=[("oT", c, tg)])

            NIT = 32
            stageY(0)
            stageA(0)
            stageAd(0)
            stageA(1)
            stageAd(1)
            stageA2(0)
            stageA(2)
            stageAd(2)
            stageA2(1)
            stageB(0)
            for n in range(NIT):
                if n + 3 < NIT:
                    if (n + 3) % 4 == 0:
                        stageY((n + 3) // 4)
                    stageA(n + 3)
                if n + 2 < NIT:
                    stageA2(n + 2)
                if n + 1 < NIT:
                    stageB(n + 1)
                stageC(n)
                if n + 3 < NIT:
                    stageAd(n + 3)
            out_proj(oT, ("b_o", l))

        OT_OFF = ARENA_W - 8192
        SPEC_NORM = [("sq", (8, 512), BF16, 6144), ("rstd0", (512,), F32), ("rstd1", (512,), F32),
                     ("stmp0", (512,), F32), ("stmp1", (512,), F32)]
        SPEC_MLP = [("z", (4, 2048), BF16), ("rtmp", (4, 512), F32)]
        SPEC_ATT = [("oT", (8, 2048), BF16, OT_OFF), ("QT0", (2048,), BF16, 0), ("KA0", (2048,), BF16), ("KB0", (2048,), BF16), ("Vx0", (16, 192), BF16),
                    ("QT1", (2048,), BF16), ("KA1", (2048,), BF16), ("KB1", (2048,), BF16), ("Vx1", (16, 192), BF16),
                    ("expS0", (512,), BF16), ("expS1", (512,), BF16), ("expS2", (512,), BF16), ("expS3", (512,), BF16),
                    ("rc0", (512,), F32), ("rc1", (512,), F32), ("relb0", (2, 256), F32),
                    ("expB0", (2, 256), BF16), ("expB1", (2, 256), BF16)]
        SPEC_REC = [("oT", (8, 2048), BF16, OT_OFF), ("yb0", (2048,), BF16, 0), ("yb1", (2048,), BF16), ("xh0", (516,), F32), ("xh1", (516,), F32),
                    ("xc0", (512,), F32), ("xc1", (512,), F32), ("xc2", (512,), F32), ("xc3", (512,), F32), ("xcb0", (512,), BF16), ("xcb1", (512,), BF16),
                    ("ab0", (512,), F32), ("ab1", (512,), F32), ("mb0", (512,), F32), ("mb1", (512,), F32),
                    ("tb0", (512,), F32), ("tb1", (512,), F32), ("tb2", (512,), F32)]

        for l in layers:
            if not DEBUG.get("skip_mix"):
                carve(SPEC_NORM)
                rmsnorm(("gmix", l))
                if l % 2 == 0:
                    carve(SPEC_ATT)
                    attention(l)
                else:
                    carve(SPEC_REC)
                    recurrent(l)
            else:
                for _ in range(10):
                    wnext(); wdone()
            if not DEBUG.get("skip_mlp"):
                carve(SPEC_NORM)
                rmsnorm(("gmlp", l))
                carve(SPEC_MLP)
                mlp()
            else:
                for _ in range(16):
                    wnext(); wdone()
        if final:
            carve(SPEC_NORM)
            rmsnorm("gfin", to_h=True)
        for tg in range(NTG):
            for hf in range(2):
                S.add("sp", lambda e, tg=tg, hf=hf: e.dma_start(out=outT_v[:, hf * 4:(hf + 1) * 4, tg * TG:(tg + 1) * TG], in_=h[:, hf * 4:(hf + 1) * 4, tg * TG:(tg + 1) * TG]),
                      reads=[("h", dc, tg) for dc in range(hf * 4, (hf + 1) * 4)], chan="hout%d" % (tg * 2 + hf))
        assert state["next_use"] == n_pieces, (state["next_use"], n_pieces)
        S.finalize(st)
        with nc.Block() as block:
            S.emit(block)
    return nc


_CACHE = {}


def run_layers(inp, x, layers, final, n_cores=None):
    B = x.shape[0]
    wbig, wsm, relb, wgate, sm_off = prep_weights(inp, layers, final)
    nc = build_program(layers, final, wbig.shape[0], wsm.shape[1], relb.shape[0], wgate.shape[0], sm_off)
    in_maps = []
    for b in range(B):
        in_maps.append({"xT": np.ascontiguousarray(x[b].T), "wbig": wbig, "wsm": wsm, "relb": relb, "wgate": wgate})
    res = run_bass_kernel_spmd(nc, in_maps, core_ids=list(range(B)))
    out = np.stack([np.ascontiguousarray(r["outT"].T) for r in res.results], axis=0)
    return out


def kernel(**inputs):
    inp = {k: np.asarray(v, dtype=np.float32) for k, v in inputs.items()}
    x = inp["x"]
    out = run_layers(inp, x, [0, 1, 2, 3], True)
    return out.astype(np.float32)
```

```python
from contextlib import ExitStack
import numpy as np
import concourse.bass as bass
import concourse.mybir as mybir
from concourse.bass_utils import run_bass_kernel_spmd

F32 = mybir.dt.float32
BF16 = mybir.dt.bfloat16
AF = mybir.ActivationFunctionType
ALU = mybir.AluOpType

D = 1024
SEQ = 2048
NTG = 4
TG = 512
NDC = 8
NSLOT = 3
SLOT_N = 4096
NEG = -30000.0
EPS = 1e-6
DEBUG = {}


class _Op:
    __slots__ = ("idx", "eng", "fn", "deps", "chan", "sig", "waits", "has_dep")


class Sched:
    def __init__(self, nc, same_engine_sync=True):
        self.nc = nc
        self.ops = []
        self.last_writer = {}
        self.readers = {}
        self.same_engine_sync = same_engine_sync

    def add(self, eng, fn, reads=(), writes=(), ex=(), chan=None):
        op = _Op()
        op.idx = len(self.ops)
        op.eng = eng
        op.fn = fn
        op.chan = chan
        op.sig = None
        op.has_dep = False
        reads = list(reads)
        writes = list(writes) + list(ex)
        deps = set()
        for k in reads:
            w = self.last_writer.get(k)
            if w is not None:
                deps.add(w)
        for k in writes:
            w = self.last_writer.get(k)
            if w is not None:
                deps.add(w)
            rd = self.readers.get(k)
            if rd:
                for v in rd.values():
                    if isinstance(v, list):
                        deps.update(v)
                    else:
                        deps.add(v)
        for k in reads:
            rd = self.readers.setdefault(k, {})
            if chan is not None:
                rd.setdefault(("dma", eng), []).append(op.idx)
            else:
                rd[eng] = op.idx
        for k in writes:
            self.last_writer[k] = op.idx
            self.readers[k] = {}
        deps.discard(op.idx)
        op.deps = deps
        self.ops.append(op)
        return op

    def finalize(self, stack):
        nc = self.nc
        ops = self.ops
        for op in ops:
            keep = set()
            for d in op.deps:
                dop = ops[d]
                if dop.chan is None and op.chan is None and dop.eng == op.eng:
                    if dop.eng == "pe" or not self.same_engine_sync:
                        continue
                keep.add(d)
            op.deps = keep
            for d in keep:
                ops[d].has_dep = True
        sems = {}
        counts = {}

        def get_sem(name):
            if name not in sems:
                sems[name] = stack.enter_context(nc.semaphore("s_" + name))
                counts[name] = 0

        for op in ops:
            if op.chan is not None:
                nm = "c_" + op.chan
                get_sem(nm)
                counts[nm] += 16
                op.sig = (nm, counts[nm])
            elif op.has_dep:
                nm = "e_" + op.eng
                get_sem(nm)
                counts[nm] += 1
                op.sig = (nm, counts[nm])
        waited = {}
        for op in ops:
            need = {}
            for d in op.deps:
                s, v = ops[d].sig
                if v > need.get(s, 0):
                    need[s] = v
            w = waited.setdefault(op.eng, {})
            op.waits = []
            for s, v in need.items():
                if w.get(s, 0) < v:
                    w[s] = v
                    op.waits.append((s, v))
        self.sems = sems
        self.counts = counts
        per_eng = {}
        for op in ops:
            per_eng.setdefault(op.eng, []).append(op)
        self.per_eng = per_eng

    def emit(self, block):
        sems = self.sems
        per_eng = self.per_eng
        counts = self.counts

        def run(engname, e):
            for op in per_eng.get(engname, []):
                for s, v in op.waits:
                    e.wait_ge(sems[s], v)
                ins = op.fn(e)
                if op.sig is not None:
                    ins.then_inc(sems[op.sig[0]], 16 if op.chan is not None else 1)
            if engname == "sp":
                for s, v in counts.items():
                    if v > 0:
                        e.wait_ge(sems[s], v)

        @block.sync
        def _(e):
            run("sp", e)

        @block.tensor
        def _(e):
            run("pe", e)

        @block.scalar
        def _(e):
            run("act", e)

        @block.vector
        def _(e):
            run("dve", e)

        @block.gpsimd
        def _(e):
            run("pool", e)


def _piece(w_sub):
    k, cols = w_sub.shape
    nch = k // 128
    out = np.ascontiguousarray(w_sub.reshape(nch, 128, cols).transpose(1, 0, 2)).reshape(128, nch * cols)
    if out.shape[1] < SLOT_N:
        pad = np.zeros((128, SLOT_N - out.shape[1]), np.float32)
        out = np.concatenate([out, pad], axis=1)
    return out


def _col(v):
    return np.ascontiguousarray(v.reshape(-1, 128).T)


def layer_kinds(layers):
    return ["attn" if l % 2 == 0 else "rec" for l in layers]


def prep_weights(inp, layers, final):
    pieces = []
    sm = []
    sm_off = {}

    def add_sm(name, arr):
        sm_off[name] = sum(a.shape[1] for a in sm)
        sm.append(np.ascontiguousarray(arr, dtype=np.float32))

    relb = []
    gates = []
    maskc = np.zeros((128, 256), np.float32)
    maskc[64:128, 0:64] = NEG
    add_sm("maskc", maskc)
    kq = np.arange(128)
    rel0 = kq[None, :] - kq[:, None]
    idx_d0 = np.clip(rel0, -128, 128) + 128
    idx_d1 = np.clip(rel0 + 128, -128, 128) + 128
    for l in layers:
        j = l // 2
        add_sm(("gmix", l), _col(inp["norm_mix"][l]))
        add_sm(("gmlp", l), _col(inp["norm_mlp"][l]))
        if l % 2 == 0:
            wqkv = inp["attn_w_qkv"][j]
            for p in range(8):
                sub = np.concatenate([wqkv[:, p * 128:(p + 1) * 128],
                                      wqkv[:, 1024 + p * 128:1024 + (p + 1) * 128],
                                      wqkv[:, 2048 + p * 128:2048 + (p + 1) * 128]], axis=1)
                pieces.append(_piece(sub))
                tab = inp["attn_rel_bias"][j]
                rb = np.zeros((128, 512), np.float32)
                for hh in range(2):
                    t = tab[2 * p + hh]
                    rb[:, hh * 256:hh * 256 + 128] = t[idx_d0]
                    rb[:, hh * 256 + 128:hh * 256 + 256] = t[idx_d1]
                relb.append(rb)
            wo = inp["attn_w_o"][j]
            for hf in range(2):
                pieces.append(_piece(wo[:, hf * 512:(hf + 1) * 512]))
            add_sm(("chcol", l), np.broadcast_to(inp["attn_rel_bias"][j][:, 256].reshape(1, 16), (128, 16)))
        else:
            ga = inp["rec_w_ga"][j]
            gx = inp["rec_w_gx"][j]
            gp = np.concatenate([ga.transpose(1, 0, 2).reshape(128, 1024), gx.transpose(1, 0, 2).reshape(128, 1024)], axis=1)
            gates.append(np.ascontiguousarray(gp))
            win = inp["rec_w_in"][j]
            for c in range(8):
                sub = np.concatenate([win[:, c * 128:(c + 1) * 128], win[:, 1024 + c * 128:1024 + (c + 1) * 128]], axis=1)
                pieces.append(_piece(sub))
            wo = inp["rec_w_o"][j]
            for hf in range(2):
                pieces.append(_piece(wo[:, hf * 512:(hf + 1) * 512]))
            add_sm(("b_in", l), _col(inp["rec_b_in"][j]))
            add_sm(("conv_w", l), np.concatenate([_col(inp["rec_conv_w"][j][t]) for t in range(4)], axis=1))
            add_sm(("conv_b", l), _col(inp["rec_conv_b"][j]))
            add_sm(("b_ga", l), _col(inp["rec_b_ga"][j].reshape(-1)))
            add_sm(("b_gx", l), _col(inp["rec_b_gx"][j].reshape(-1)))
            add_sm(("a_param", l), _col(inp["rec_a_param"][j]))
            add_sm(("b_o", l), _col(inp["rec_b_o"][j]))
        w1 = inp["mlp_w1"][l]
        w2 = inp["mlp_w2"][l]
        for g in range(8):
            pieces.append(_piece(w1[:, g * 512:(g + 1) * 512]))
            pieces.append(_piece(w2[g * 512:(g + 1) * 512, :]))
    if final:
        add_sm("gfin", _col(inp["norm_final"]))
    wbig = np.ascontiguousarray(np.stack(pieces, axis=0))
    wsm = np.ascontiguousarray(np.concatenate(sm, axis=1))
    if not relb:
        relb = [np.zeros((128, 512), np.float32)]
    relb = np.ascontiguousarray(np.stack(relb, axis=0))
    if not gates:
        gates = [np.zeros((128, 2048), np.float32)]
    wgate = np.ascontiguousarray(np.stack(gates, axis=0))
    return wbig, wsm, relb, wgate, sm_off


def build_program(layers, final, n_pieces, n_sm, n_relb, n_gate, sm_off):
    nc = bass.Bass("TRN2", target_bir_lowering=False)
    xT = nc.dram_tensor("xT", [D, SEQ], F32, kind="ExternalInput").ap()
    wbig = nc.dram_tensor("wbig", [n_pieces, 128, SLOT_N], F32, kind="ExternalInput").ap()
    wsm_d = nc.dram_tensor("wsm", [128, n_sm], F32, kind="ExternalInput").ap()
    relb_d = nc.dram_tensor("relb", [n_relb, 128, 512], F32, kind="ExternalInput").ap()
    wgate_d = nc.dram_tensor("wgate", [n_gate, 128, 2048], F32, kind="ExternalInput").ap()
    outT = nc.dram_tensor("outT", [D, SEQ], F32, kind="ExternalOutput").ap()
    xT_v = xT.rearrange("(c p) t -> p c t", p=128)
    outT_v = outT.rearrange("(c p) t -> p c t", p=128)

    with ExitStack() as st:
        def sb(name, shape, dt):
            return st.enter_context(nc.sbuf_tensor("k_" + name, shape, dt))

        h = sb("h", [128, NDC, SEQ], F32)
        hn = sb("hn", [128, NDC, SEQ], BF16)
        wsl = sb("wsl", [128, NSLOT, SLOT_N], BF16)
        wsm = sb("wsm_sb", [128, n_sm], F32)
        ones = sb("ones", [128, 128], BF16)
        dummy = sb("dummy", [128, 8], F32)
        rder = sb("rder", [128, 40], F32)
        gatebuf = sb("gatebuf", [128, 2048], BF16)
        ARENA_W = 20480
        arena = sb("arena", [128, ARENA_W], F32)
        ps = st.enter_context(nc.psum_tensor("k_ps", [128, 8, 512], F32))
        arena_t = {}

        def carve(spec):
            arena_t.clear()
            off = 0
            for ent in spec:
                name, shape, dt = ent[0], ent[1], ent[2]
                if len(ent) > 3:
                    off = ent[3]
                n = int(np.prod(shape))
                words = n // 2 if dt == BF16 else n
                v = arena[:, off:off + words]
                if dt == BF16:
                    v = v.bitcast(BF16)
                if len(shape) == 2:
                    v = v.rearrange("p (a b) -> p a b", b=shape[1])
                elif len(shape) == 3:
                    v = v.rearrange("p (a b c) -> p a b c", b=shape[1], c=shape[2])
                arena_t[name] = v
                off += words
            assert off <= ARENA_W, (off, ARENA_W)

        S = Sched(nc)
        AE = ["AE_A", "AE_B", "AE_C"]

        def arena_join(keys):
            S.add("pool", lambda e: e.memset(dummy[:, 0:1], 0.0), writes=list(keys) + ["dummy"])

        state = {"next_load": 0, "next_use": 0}

        def wload():
            k = state["next_load"]
            if k >= n_pieces:
                return
            state["next_load"] += 1
            slot = k % NSLOT
            src = wbig[k].rearrange("p (a b) -> p a b", b=1024)
            dst = wsl[:, slot, :].rearrange("p (a b) -> p a b", b=1024)
            S.add("pool", lambda e: e.dma_start(out=dst, in_=src), writes=[("w", slot)], chan="w%d" % slot)

        def wnext():
            k = state["next_use"]
            state["next_use"] += 1
            assert k < state["next_load"], "piece used before load emitted"
            return k % NSLOT

        def wdone():
            wload()

        def smc(name, i=0, n=1):
            o = sm_off[name] + i
            return wsm[:, o:o + n]

        S.add("sp", lambda e: e.dma_start(out=wsm[:], in_=wsm_d[:, :]), writes=["wsm"], chan="wsm")
        for tg in range(NTG):
            for hf in range(2):
                S.add("sp", lambda e, tg=tg, hf=hf: e.dma_start(out=h[:, hf * 4:(hf + 1) * 4, tg * TG:(tg + 1) * TG], in_=xT_v[:, hf * 4:(hf + 1) * 4, tg * TG:(tg + 1) * TG]),
                      writes=[("h", dc, tg) for dc in range(hf * 4, (hf + 1) * 4)], chan="hin%d" % (tg * 2 + hf))
        S.add("dve", lambda e: e.memset(ones[:], 1.0), writes=["ones"])
        for _ in range(NSLOT):
            wload()

        def rmsnorm(gname, to_h=False):
            AE[:] = ["AE_B"]
            arena_join(["AE_B"])
            sq = arena_t["sq"]
            rstd = [arena_t["rstd0"], arena_t["rstd1"]]
            stmp = [arena_t["stmp0"], arena_t["stmp1"]]
            for tg in range(NTG):
                par = tg % 2
                tsl = slice(tg * TG, (tg + 1) * TG)
                bank = par
                for dc in range(NDC):
                    S.add("act", lambda e, dc=dc, tsl=tsl: e.activation(out=sq[:, dc, :], in_=h[:, dc, tsl], func=AF.Square),
                          reads=[("h", dc, tg), *AE], writes=[("sq", dc)])
                for dc in range(NDC):
                    S.add("pe", lambda e, dc=dc, bank=bank: e.matmul(ps[:, bank, :], lhsT=ones[:], rhs=sq[:, dc, :], start=(dc == 0), stop=(dc == NDC - 1)),
                          reads=[("sq", dc), "ones", *AE], ex=[("ps", bank)])
                if to_h:
                    S.add("act", lambda e, par=par, bank=bank: e.activation(out=stmp[par][:], in_=ps[:, bank, :], func=AF.Sqrt, scale=1.0 / D, bias=EPS),
                          reads=[*AE], writes=[("stmp", par)], ex=[("ps", bank)])
                    S.add("dve", lambda e, par=par: e.reciprocal(out=rstd[par][:], in_=stmp[par][:]),
                          reads=[("stmp", par), *AE], writes=[("rstd", par)])
                else:
                    S.add("act", lambda e, par=par, bank=bank: e.activation(out=stmp[par][:], in_=ps[:, bank, :], func=AF.Ln, scale=1.0 / D, bias=EPS),
                          reads=[*AE], writes=[("stmp", par)], ex=[("ps", bank)])
                    S.add("act", lambda e, par=par: e.activation(out=rstd[par][:], in_=stmp[par][:], func=AF.Exp, scale=-0.5),
                          reads=[("stmp", par), *AE], writes=[("rstd", par)])
                for dc in range(NDC):
                    if to_h:
                        S.add("dve", lambda e, dc=dc, par=par, tsl=tsl: e.scalar_tensor_tensor(out=h[:, dc, tsl], in0=h[:, dc, tsl], scalar=smc(gname, dc), in1=rstd[par][:], op0=ALU.mult, op1=ALU.mult),
                              reads=[("h", dc, tg), ("rstd", par), "wsm", *AE], writes=[("h", dc, tg)])
                    else:
                        S.add("dve", lambda e, dc=dc, par=par, tsl=tsl: e.scalar_tensor_tensor(out=hn[:, dc, tsl], in0=h[:, dc, tsl], scalar=smc(gname, dc), in1=rstd[par][:], op0=ALU.mult, op1=ALU.mult),
                              reads=[("h", dc, tg), ("rstd", par), "wsm", *AE], writes=[("hn", dc, tg)])

        def mlp():
            AE[:] = ["AE_A"]
            arena_join(["AE_A"])
            z = arena_t["z"]
            rtmp = arena_t["rtmp"]
            cnt = {"a": 0, "b": 0}
            for g in range(8):
                s1 = wnext()
                w1v = wsl[:, s1, :].rearrange("p (c f) -> p c f", f=512)
                for tg in range(NTG):
                    tsl = slice(tg * TG, (tg + 1) * TG)
                    for fc in range(4):
                        bank = cnt["a"] % 4
                        rt = cnt["a"] % 4
                        cnt["a"] += 1
                        for dc in range(NDC):
                            S.add("pe", lambda e, dc=dc, fc=fc, bank=bank, tsl=tsl, w1v=w1v: e.matmul(ps[:, bank, :], lhsT=w1v[:, dc, fc * 128:(fc + 1) * 128], rhs=hn[:, dc, tsl], start=(dc == 0), stop=(dc == NDC - 1)),
                                  reads=[("w", s1), ("hn", dc, tg)], ex=[("ps", bank)])
                        S.add("act", lambda e, bank=bank, rt=rt: e.activation(out=rtmp[:, rt, :], in_=ps[:, bank, :], func=AF.Relu),
                              reads=[*AE], writes=[("rtmp", rt)], ex=[("ps", bank)])
                        S.add("act", lambda e, rt=rt, fc=fc, tsl=tsl: e.activation(out=z[:, fc, tsl], in_=rtmp[:, rt, :], func=AF.Square),
                              reads=[("rtmp", rt), *AE], writes=[("z", fc, tg)])
                wdone()
                s2 = wnext()
                w2v = wsl[:, s2, :].rearrange("p (c f) -> p c f", f=1024)
                for tg in range(NTG):
                    tsl = slice(tg * TG, (tg + 1) * TG)
                    for dcs in range(NDC):
                        bank = 4 + cnt["b"] % 4
                        cnt["b"] += 1
                        for fc in range(4):
                            S.add("pe", lambda e, fc=fc, dcs=dcs, bank=bank, tsl=tsl, w2v=w2v: e.matmul(ps[:, bank, :], lhsT=w2v[:, fc, dcs * 128:(dcs + 1) * 128], rhs=z[:, fc, tsl], start=(fc == 0), stop=(fc == 3)),
                                  reads=[("w", s2), ("z", fc, tg), *AE], ex=[("ps", bank)])
                        S.add("dve", lambda e, dcs=dcs, bank=bank, tsl=tsl: e.tensor_tensor(out=h[:, dcs, tsl], in0=ps[:, bank, :], in1=h[:, dcs, tsl], op=ALU.add),
                              reads=[("h", dcs, tg)], writes=[("h", dcs, tg)], ex=[("ps", bank)])
                wdone()

        def out_proj(oT, bias_name=None):
            AE[:] = ["AE_C"]
            cnt = 0
            ss = [wnext(), wnext()]
            wvs = [wsl[:, sx, :].rearrange("p (c f) -> p c f", f=512) for sx in ss]
            for tg in range(NTG):
                tsl = slice(tg * TG, (tg + 1) * TG)
                for hf in range(2):
                    s_, wv = ss[hf], wvs[hf]
                    for dcs in range(4):
                        dco = hf * 4 + dcs
                        bank = 6 + cnt % 2
                        cnt += 1
                        for pc in range(NDC):
                            S.add("pe", lambda e, pc=pc, dcs=dcs, bank=bank, tsl=tsl, wv=wv: e.matmul(ps[:, bank, :], lhsT=wv[:, pc, dcs * 128:(dcs + 1) * 128], rhs=oT[:, pc, tsl], start=(pc == 0), stop=(pc == NDC - 1)),
                                  reads=[("w", s_), ("oT", pc, tg), *AE], ex=[("ps", bank)])
                        if bias_name is None:
                            S.add("dve", lambda e, dco=dco, bank=bank, tsl=tsl: e.tensor_tensor(out=h[:, dco, tsl], in0=ps[:, bank, :], in1=h[:, dco, tsl], op=ALU.add),
                                  reads=[("h", dco, tg)], writes=[("h", dco, tg)], ex=[("ps", bank)])
                        else:
                            S.add("dve", lambda e, dco=dco, bank=bank, tsl=tsl: e.scalar_tensor_tensor(out=h[:, dco, tsl], in0=ps[:, bank, :], scalar=smc(bias_name, dco), in1=h[:, dco, tsl], op0=ALU.add, op1=ALU.add),
                                  reads=[("h", dco, tg), "wsm"], writes=[("h", dco, tg)], ex=[("ps", bank)])
            wdone()
            wdone()

        relb_state = {"n": 0}

        def attention(l):
            AE[:] = ["AE_A", "AE_B"]
            arena_join(["AE_A", "AE_B", "AE_C"])
            oT = arena_t["oT"]
            QT = [arena_t["QT0"], arena_t["QT1"]]
            KAB = [[arena_t["KA0"], arena_t["KB0"]], [arena_t["KA1"], arena_t["KB1"]]]
            expB = [arena_t["expB0"], arena_t["expB1"]]
            Vx = [arena_t["Vx0"], arena_t["Vx1"]]
            expS = [arena_t["expS%d" % i] for i in range(4)]
            rc = [arena_t["rc0"], arena_t["rc1"]]
            rbuf = [arena_t["relb0"], arena_t["relb0"]]
            Bm = arena_t["relb0"]
            for b in range(2):
                S.add("pool", lambda e, b=b: e.memset(Vx[b][:, :, 64:128], 1.0), reads=[*AE], writes=[("Vx_ones", b)])
                S.add("pool", lambda e, b=b: e.memset(KAB[b][1][0:64, :], 0.0), reads=[*AE], writes=[("Kz", b)])
            cnts = {"ss0": 0, "ss1": 0, "po": 0, "pj": 0}
            PJ_BANKS = [6, 7]

            def load_relb(p):
                gi = relb_state["n"]
                relb_state["n"] += 1
                S.add("sp", lambda e, gi=gi, p=p: e.dma_start(out=rbuf[p % 2][:].rearrange("p a b -> p (a b)"), in_=relb_d[gi]), reads=[*AE], writes=[("relb", 0)], chan="relb0")

            def proj_units(p, banks):
                pb = p % 2
                s = wnext()
                wv = wsl[:, s, 0:8 * 384].rearrange("p (c f) -> p c f", f=384)
                units = []
                for which in (0, 1):
                    for tg in range(NTG):
                        ubank = {}

                        def ua(which=which, tg=tg, ubank=ubank):
                            tsl = slice(tg * TG, (tg + 1) * TG)
                            bank = banks[cnts["pj"] % len(banks)]
                            cnts["pj"] += 1
                            ubank[0] = bank
                            for dc in range(NDC // 2):
                                S.add("pe", lambda e, dc=dc: e.matmul(ps[:, bank, :], lhsT=wv[:, dc, which * 128:(which + 1) * 128], rhs=hn[:, dc, tsl], start=(dc == 0), stop=False),
                                      reads=[("w", s), ("hn", dc, tg)], ex=[("ps", bank)])
                        units.append(ua)

                        def u(which=which, tg=tg, ubank=ubank):
                            tsl = slice(tg * TG, (tg + 1) * TG)
                            bank = ubank[0]
                            for dc in range(NDC // 2, NDC):
                                S.add("pe", lambda e, dc=dc: e.matmul(ps[:, bank, :], lhsT=wv[:, dc, which * 128:(which + 1) * 128], rhs=hn[:, dc, tsl], start=False, stop=(dc == NDC - 1)),
                                      reads=[("w", s), ("hn", dc, tg)], ex=[("ps", bank)])
                            if which == 0:
                                S.add("dve", lambda e: e.tensor_scalar_mul(out=QT[pb][:, tsl], in0=ps[:, bank, :], scalar1=0.125),
                                      reads=[*AE], writes=[("qk", pb, 0, tg)], ex=[("ps", bank)])
                            else:
                                S.add("dve", lambda e: e.tensor_copy(out=KAB[pb][0][:, tsl], in_=ps[:, bank, :]),
                                      reads=[*AE], writes=[("qk", pb, 1, tg)], ex=[("ps", bank)])
                                S.add("pool", lambda e: e.tensor_copy(out=KAB[pb][1][64:128, tsl], in_=KAB[pb][0][64:128, tsl]),
                                      reads=[("qk", pb, 1, tg), *AE], writes=[("qk", pb, 2, tg)])
                                S.add("pool", lambda e: e.memset(KAB[pb][0][64:128, tsl], 0.0),
                                      reads=[*AE], writes=[("qk", pb, 1, tg)])
                        units.append(u)
                vbank = {}
                for t in range(16):
                    def u(t=t):
                        tq, t4 = t // 4, t % 4
                        if t4 == 0:
                            vbank[tq] = banks[cnts["pj"] % len(banks)]
                            cnts["pj"] += 1
                        bank = vbank[tq]
                        for dc in range(NDC):
                            S.add("pe", lambda e, dc=dc: e.matmul(ps[:, bank, t4 * 128:(t4 + 1) * 128], lhsT=hn[:, dc, t * 128:(t + 1) * 128], rhs=wv[:, dc, 256:384], start=(dc == 0), stop=(dc == NDC - 1)),
                                  reads=[("w", s), ("hn", dc, t // 4)], ex=[("ps", bank)])
                        if t4 == 3:
                            psv = ps[:, bank, :].rearrange("p (t c) -> p t c", c=128)
                            S.add("dve", lambda e: e.tensor_copy(out=Vx[pb][:, tq * 4:(tq + 1) * 4, 0:64], in_=psv[:, :, 0:64]),
                                  reads=[*AE], writes=[("V", pb, tq, 0)], ex=[("ps", bank)])
                            S.add("dve", lambda e: e.tensor_copy(out=Vx[pb][:, tq * 4:(tq + 1) * 4, 128:192], in_=psv[:, :, 64:128]),
                                  reads=[*AE], writes=[("V", pb, tq, 1)], ex=[("ps", bank)])
                    units.append(u)
                units.append(wdone)
                return units

            def prep_bias(p):
                for hh in range(2):
                    S.add("dve", lambda e, hh=hh, p=p: e.scalar_tensor_tensor(out=Bm[:, hh, :], in0=Bm[:, hh, :], scalar=smc(("chcol", l), 2 * p + hh), in1=smc("maskc", 0, 256), op0=ALU.subtract, op1=ALU.add),
                          reads=["wsm", *AE], writes=[("relb", 0)])
                    S.add("act", lambda e, hh=hh, p=p: e.activation(out=expB[p % 2][:, hh, :], in_=Bm[:, hh, :], func=AF.Exp),
                          reads=[("relb", 0), *AE], writes=[("expB", p % 2, hh)])
                if p + 1 < 8:
                    load_relb(p + 1)

            load_relb(0)
            for u in proj_units(0, [0, 1, 2, 3, 7]):
                u()
            prep_bias(0)
            deferred = []
            for p in range(8):
                pb = p % 2
                if p + 1 < 8:
                    units = proj_units(p + 1, PJ_BANKS)
                else:
                    units = []
                items = []
                for G in range(4):
                    for t in range(max(0, 4 * G - 4), 4 * G + 4):
                        for hh in range(2):
                            items.append((G, t, hh))
                nitems = len(items)
                ntot = max(len(units), 1)
                sset_of = {}

                def geom(G, t):
                    j0 = max(t, 4 * G)
                    j1 = min(t + 4, 4 * G + 3)
                    return j0, j1, (j1 - j0 + 1) * 128

                def scores(item, p=p, pb=pb):
                    G, t, hh = item
                    r0 = hh * 64
                    j0, j1, N = geom(G, t)
                    k = cnts["ss0"]
                    cnts["ss0"] += 1
                    es = expS[k % 4]
                    ek = k % 4
                    sset_of[item] = (es, ek)
                    bank = k % 2
                    S.add("pe", lambda e: e.matmul(ps[:, bank, 0:N], lhsT=KAB[pb][hh][:, t * 128:(t + 1) * 128], rhs=QT[pb][:, j0 * 128:(j1 + 1) * 128], start=True, stop=True),
                          reads=[("qk", pb, 0, G), ("qk", pb, 1, t // 4), ("qk", pb, 2, t // 4), ("Kz", pb), *AE], ex=[("ps", bank)])
                    S.add("act", lambda e: e.activation(out=es[:, 0:N], in_=ps[:, bank, 0:N], func=AF.Exp),
                          reads=[*AE], writes=[("expS", ek, 0), ("expS", ek, 1)], ex=[("ps", bank)])
                    if j1 == t + 4:
                        S.add("dve", lambda e: e.memset(es[0:64, N - 64:N], 0.0), reads=[*AE], writes=[("expS", ek, 1)])
                    d0 = j0 - t
                    if d0 == 0:
                        nn = min(2, (j1 - j0 + 1))
                        S.add("dve", lambda e: e.tensor_tensor(out=es[:, 0:nn * 128], in0=es[:, 0:nn * 128], in1=expB[p % 2][:, hh, 0:nn * 128], op=ALU.mult),
                              reads=[("expB", p % 2, hh), *AE], writes=[("expS", ek, 0)])
                    elif d0 == 1:
                        S.add("dve", lambda e: e.tensor_tensor(out=es[:, 0:128], in0=es[:, 0:128], in1=expB[p % 2][:, hh, 128:256], op=ALU.mult),
                              reads=[("expB", p % 2, hh), *AE], writes=[("expS", ek, 0)])

                def pv(item, p=p, pb=pb):
                    G, t, hh = item
                    j0, j1, N = geom(G, t)
                    es, ek = sset_of[item]
                    vcols = slice(0, 128) if hh == 0 else slice(64, 192)
                    so = slice(64, 128) if hh == 0 else slice(0, 64)
                    oo = slice(0, 64) if hh == 0 else slice(64, 128)
                    bank = 2 + 2 * hh + G % 2
                    c0 = (j0 - 4 * G) * 128
                    first = (t == max(0, 4 * G - 4))
                    last = (t == 4 * G + 3)
                    S.add("pe", lambda e: e.matmul(ps[:, bank, c0:c0 + N], lhsT=Vx[pb][:, t, vcols], rhs=es[:, 0:N], start=first, stop=last, skip_group_check=True),
                          reads=[("expS", ek, 0), ("expS", ek, 1), ("V", pb, t // 4, 0), ("V", pb, t // 4, 1), ("Vx_ones", pb), *AE], ex=[("ps", bank)])
                    if last:
                        par = cnts["po"] % 2
                        cnts["po"] += 1
                        tsl = slice(G * TG, (G + 1) * TG)

                        def n1():
                            S.add("act", lambda e: e.activation(out=rc[par][so, :], in_=ps[so, bank, :], func=AF.Ln),
                                  reads=[*AE], writes=[("rc", par)], ex=[("ps", bank)])

                        def n2():
                            S.add("act", lambda e: e.activation(out=rc[par][so, :], in_=rc[par][so, :], func=AF.Exp, scale=-1.0),
                                  reads=[("rc", par), *AE], writes=[("rc", par)])

                        def n3():
                            S.add("dve", lambda e: e.tensor_tensor(out=oT[oo, p, tsl], in0=ps[oo, bank, :], in1=rc[par][so, :], op=ALU.mult),
                                  reads=[("rc", par), "AE_C", *AE], writes=[("oT", p, G)], ex=[("ps", bank)])
                        deferred.append([2, n1])
                        deferred.append([3, n2])
                        deferred.append([4, n3])

                LA = 3
                for i in range(nitems + LA):
                    if i == nitems // 2 and p + 1 < 8:
                        prep_bias(p + 1)
                    if i < nitems:
                        scores(items[i])
                    for d in deferred:
                        d[0] -= 1
                    while deferred and deferred[0][0] <= 0:
                        deferred.pop(0)[1]()
                    if i >= LA:
                        pv(items[i - LA])
                        step = i - LA + 1
                        while units and len(units) * nitems > (nitems - step) * ntot:
                            units.pop(0)()
                while units:
                    units.pop(0)()
            while deferred:
                deferred.pop(0)[1]()
            out_proj(oT, None)

        gate_state = {"n": 0}

        def recurrent(l):
            AE[:] = ["AE_A", "AE_B"]
            arena_join(["AE_A", "AE_B", "AE_C"])
            oT = arena_t["oT"]
            yb = [arena_t["yb0"], arena_t["yb1"]]
            xh = [arena_t["xh0"], arena_t["xh1"]]
            xc = [arena_t["xc0"], arena_t["xc1"], arena_t["xc2"], arena_t["xc3"]]
            xcb = [arena_t["xcb0"], arena_t["xcb1"]]
            ab = [arena_t["ab0"], arena_t["ab1"]]
            mb = [arena_t["mb0"], arena_t["mb1"]]
            tb = [arena_t["tb0"], arena_t["tb1"], arena_t["tb2"]]
            S.add("act", lambda e: e.activation(out=rder[:, 32:40], in_=smc(("a_param", l), 0, 8), func=AF.Exp, scale=-1.0),
                  reads=["wsm"], writes=["rder_t"])
            S.add("act", lambda e: e.activation(out=rder[:, 32:40], in_=rder[:, 32:40], func=AF.Ln, bias=1.0),
                  reads=["rder_t"], writes=["rder_t"])
            S.add("dve", lambda e: e.tensor_scalar_mul(out=rder[:, 0:8], in0=rder[:, 32:40], scalar1=-8.0), reads=["rder_t"], writes=["rder"])
            S.add("dve", lambda e: e.tensor_scalar_mul(out=rder[:, 8:16], in0=rder[:, 32:40], scalar1=-4.0), reads=["rder_t"], writes=["rder"])
            S.add("dve", lambda e: e.tensor_scalar_mul(out=rder[:, 16:24], in0=smc(("b_ga", l), 0, 8), scalar1=0.5), reads=["wsm"], writes=["rder"])
            S.add("dve", lambda e: e.tensor_scalar_mul(out=rder[:, 24:32], in0=smc(("b_gx", l), 0, 8), scalar1=0.5), reads=["wsm"], writes=["rder"])
            gi = gate_state["n"]
            gate_state["n"] += 1
            S.add("pool", lambda e, gi=gi: e.dma_start(out=gatebuf[:].rearrange("p (a b) -> p a b", b=1024), in_=wgate_d[gi].rearrange("p (a b) -> p a b", b=1024)),
                  writes=["gate"], chan="gate")
            wg = gatebuf[:].rearrange("p (g n e) -> p g n e", g=2, n=8)
            cnt = {"pj": 0}
            wstate = {}

            def wv_of(c):
                if c not in wstate:
                    s = wnext()
                    wstate[c] = (s, wsl[:, s, 0:2048].rearrange("p (c f) -> p c f", f=256))
                return wstate[c]

            def stageY(c):
                s, wv = wv_of(c)
                for tg in range(NTG):
                    tsl = slice(tg * TG, (tg + 1) * TG)
                    bank = cnt["pj"] % 4
                    cnt["pj"] += 1
                    for dc in range(NDC):
                        S.add("pe", lambda e, dc=dc, bank=bank, tsl=tsl: e.matmul(ps[:, bank, :], lhsT=wv[:, dc, 128:256], rhs=hn[:, dc, tsl], start=(dc == 0), stop=(dc == NDC - 1)),
                              reads=[("w", s), ("hn", dc, tg)], ex=[("ps", bank)])
                    S.add("act", lambda e, bank=bank, tsl=tsl: e.activation(out=yb[c % 2][:, tsl], in_=ps[:, bank, :], func=AF.Gelu_apprx_tanh, bias=smc(("b_in", l), 8 + c)),
                          reads=["wsm", *AE], writes=[("yb", c % 2, tg)], ex=[("ps", bank)])

            def stageA(n):
                c, tg = n // 4, n % 4
                s, wv = wv_of(c)
                p2, p3 = n % 2, n % 4
                tsl = slice(tg * TG, (tg + 1) * TG)
                bank = cnt["pj"] % 4
                cnt["pj"] += 1
                for dc in range(NDC):
                    S.add("pe", lambda e, dc=dc: e.matmul(ps[:, bank, :], lhsT=wv[:, dc, 0:128], rhs=hn[:, dc, tsl], start=(dc == 0), stop=(dc == NDC - 1)),
                          reads=[("w", s), ("hn", dc, tg)], ex=[("ps", bank)])
                if tg == 3:
                    wdone()
                S.add("dve", lambda e: e.tensor_scalar_add(out=xh[p2][:, 3:515], in0=ps[:, bank, :], scalar1=smc(("b_in", l), c)),
                      reads=["wsm", *AE], writes=[("xh", p2)], ex=[("ps", bank)])
                if tg == 0:
                    S.add("pool", lambda e: e.memset(xh[p2][:, 0:3], 0.0), reads=[*AE], writes=[("xhalo", p2)])
                else:
                    S.add("pool", lambda e: e.tensor_copy(out=xh[p2][:, 0:3], in_=xh[1 - p2][:, 512:515]),
                          reads=[("xh", 1 - p2), *AE], writes=[("xhalo", p2)])
                S.add("pool", lambda e: e.tensor_scalar(out=xc[p3][:], in0=xh[p2][:, 0:512], scalar1=smc(("conv_w", l), c), scalar2=smc(("conv_b", l), c), op0=ALU.mult, op1=ALU.add),
                      reads=[("xh", p2), ("xhalo", p2), "wsm", *AE], writes=[("xc", p3)])

            def stageAd(n):
                c, tg = n // 4, n % 4
                p2, p3 = n % 2, n % 4
                for tap in range(1, 4):
                    S.add("dve", lambda e, tap=tap: e.scalar_tensor_tensor(out=xc[p3][:], in0=xh[p2][:, tap:tap + 512], scalar=smc(("conv_w", l), tap * 8 + c), in1=xc[p3][:], op0=ALU.mult, op1=ALU.add),
                          reads=[("xh", p2), ("xhalo", p2), ("xc", p3), "wsm", *AE], writes=[("xc", p3)])
                S.add("dve", lambda e: e.tensor_copy(out=xcb[p2][:], in_=xc[p3][:]), reads=[("xc", p3), *AE], writes=[("xcb", p2)])

            def stageA2(n):
                c, tg = n // 4, n % 4
                p2 = n % 2
                bank_r = 4 + 2 * p2
                bank_i = bank_r + 1
                S.add("pe", lambda e: e.matmul(ps[:, bank_r, :], lhsT=wg[:, 0, c, :], rhs=xcb[p2][:], start=True, stop=True),
                      reads=["gate", ("xcb", p2), *AE], ex=[("ps", bank_r)])
                S.add("pe", lambda e: e.matmul(ps[:, bank_i, :], lhsT=wg[:, 1, c, :], rhs=xcb[p2][:], start=True, stop=True),
                      reads=["gate", ("xcb", p2), *AE], ex=[("ps", bank_i)])

            def stageB(n):
                c, tg = n // 4, n % 4
                p2, p3 = n % 2, n % 3
                bank_r = 4 + 2 * p2
                bank_i = bank_r + 1
                S.add("act", lambda e: e.activation(out=ab[p2][:], in_=ps[:, bank_r, :], func=AF.Tanh, scale=0.5, bias=rder[:, 16 + c:17 + c]),
                      reads=["rder", *AE], writes=[("ab", p2)], ex=[("ps", bank_r)])
                S.add("act", lambda e: e.activation(out=tb[p3][:], in_=ps[:, bank_i, :], func=AF.Tanh, scale=0.5, bias=rder[:, 24 + c:25 + c]),
                      reads=["rder", *AE], writes=[("tb", p3)], ex=[("ps", bank_i)])
                S.add("act", lambda e: e.activation(out=mb[p2][:], in_=ab[p2][:], func=AF.Exp, scale=rder[:, c:c + 1], bias=rder[:, c:c + 1]),
                      reads=[("ab", p2), "rder", *AE], writes=[("mb", p2)])
                S.add("act", lambda e: e.activation(out=ab[p2][:], in_=ab[p2][:], func=AF.Exp, scale=rder[:, 8 + c:9 + c], bias=rder[:, 8 + c:9 + c]),
                      reads=[("ab", p2), "rder", *AE], writes=[("ab", p2)])
                S.add("act", lambda e: e.activation(out=mb[p2][:], in_=mb[p2][:], func=AF.Ln, scale=-1.0, bias=1.0),
                      reads=[("mb", p2), *AE], writes=[("mb", p2)])
                S.add("act", lambda e: e.activation(out=mb[p2][:], in_=mb[p2][:], func=AF.Exp, scale=0.5),
                      reads=[("mb", p2), *AE], writes=[("mb", p2)])

            def stageC(n):
                c, tg = n // 4, n % 4
                p2, p3 = n % 2, n % 3
                p4 = n % 4
                q3 = (n - 1) % 3
                tsl = slice(tg * TG, (tg + 1) * TG)
                S.add("dve", lambda e: e.scalar_tensor_tensor(out=tb[p3][:], in0=tb[p3][:], scalar=1.0, in1=xc[p4][:], op0=ALU.add, op1=ALU.mult),
                      reads=[("tb", p3), ("xc", p4), *AE], writes=[("tb", p3)])
                S.add("dve", lambda e: e.scalar_tensor_tensor(out=mb[p2][:], in0=tb[p3][:], scalar=0.5, in1=mb[p2][:], op0=ALU.mult, op1=ALU.mult),
                      reads=[("tb", p3), ("mb", p2), *AE], writes=[("mb", p2)])
                if tg == 0:
                    S.add("dve", lambda e: e.tensor_tensor_scan(out=tb[p3][:], data0=ab[p2][:], data1=mb[p2][:], initial=0.0, op0=ALU.mult, op1=ALU.add),
                          reads=[("ab", p2), ("mb", p2), ("tb", p3), *AE], writes=[("tb", p3)])
                else:
                    S.add("dve", lambda e: e.tensor_tensor_scan(out=tb[p3][:], data0=ab[p2][:], data1=mb[p2][:], initial=tb[q3][:, 511:512], op0=ALU.mult, op1=ALU.add),
                          reads=[("ab", p2), ("mb", p2), ("tb", p3), ("tb", q3), *AE], writes=[("tb", p3)])
                S.add("pool", lambda e: e.tensor_tensor(out=oT[:, c, tsl], in0=tb[p3][:], in1=yb[c % 2][:, tsl], op=ALU.mult),
                      reads=[("tb", p3), ("yb", c % 2, tg), "AE_C", *AE], writes=[("oT", c, tg)])

            NIT = 32
            stageY(0)
            stageA(0)
            stageAd(0)
            stageA(1)
            stageAd(1)
            stageA2(0)
            stageA(2)
            stageAd(2)
            stageA2(1)
            stageB(0)
            for n in range(NIT):
                if n + 3 < NIT:
                    if (n + 3) % 4 == 0:
                        stageY((n + 3) // 4)
                    stageA(n + 3)
                if n + 2 < NIT:
                    stageA2(n + 2)
                if n + 1 < NIT:
                    stageB(n + 1)
                stageC(n)
                if n + 3 < NIT:
                    stageAd(n + 3)
            out_proj(oT, ("b_o", l))

        OT_OFF = ARENA_W - 8192
        SPEC_NORM = [("sq", (8, 512), BF16, 6144), ("rstd0", (512,), F32), ("rstd1", (512,), F32),
                     ("stmp0", (512,), F32), ("stmp1", (512,), F32)]
        SPEC_MLP = [("z", (4, 2048), BF16), ("rtmp", (4, 512), F32)]
        SPEC_ATT = [("oT", (8, 2048), BF16, OT_OFF), ("QT0", (2048,), BF16, 0), ("KA0", (2048,), BF16), ("KB0", (2048,), BF16), ("Vx0", (16, 192), BF16),
                    ("QT1", (2048,), BF16), ("KA1", (2048,), BF16), ("KB1", (2048,), BF16), ("Vx1", (16, 192), BF16),
                    ("expS0", (512,), BF16), ("expS1", (512,), BF16), ("expS2", (512,), BF16), ("expS3", (512,), BF16),
                    ("rc0", (512,), F32), ("rc1", (512,), F32), ("relb0", (2, 256), F32),
                    ("expB0", (2, 256), BF16), ("expB1", (2, 256), BF16)]
        SPEC_REC = [("oT", (8, 2048), BF16, OT_OFF), ("yb0", (2048,), BF16, 0), ("yb1", (2048,), BF16), ("xh0", (516,), F32), ("xh1", (516,), F32),
                    ("xc0", (512,), F32), ("xc1", (512,), F32), ("xc2", (512,), F32), ("xc3", (512,), F32), ("xcb0", (512,), BF16), ("xcb1", (512,), BF16),
                    ("ab0", (512,), F32), ("ab1", (512,), F32), ("mb0", (512,), F32), ("mb1", (512,), F32),
                    ("tb0", (512,), F32), ("tb1", (512,), F32), ("tb2", (512,), F32)]

        for l in layers:
            if not DEBUG.get("skip_mix"):
                carve(SPEC_NORM)
                rmsnorm(("gmix", l))
                if l % 2 == 0:
                    carve(SPEC_ATT)
                    attention(l)
                else:
                    carve(SPEC_REC)
                    recurrent(l)
            else:
                for _ in range(10):
                    wnext(); wdone()
            if not DEBUG.get("skip_mlp"):
                carve(SPEC_NORM)
                rmsnorm(("gmlp", l))
                carve(SPEC_MLP)
                mlp()
            else:
                for _ in range(16):
                    wnext(); wdone()
        if final:
            carve(SPEC_NORM)
            rmsnorm("gfin", to_h=True)
        for tg in range(NTG):
            for hf in range(2):
                S.add("sp", lambda e, tg=tg, hf=hf: e.dma_start(out=outT_v[:, hf * 4:(hf + 1) * 4, tg * TG:(tg + 1) * TG], in_=h[:, hf * 4:(hf + 1) * 4, tg * TG:(tg + 1) * TG]),
                      reads=[("h", dc, tg) for dc in range(hf * 4, (hf + 1) * 4)], chan="hout%d" % (tg * 2 + hf))
        assert state["next_use"] == n_pieces, (state["next_use"], n_pieces)
        S.finalize(st)
        with nc.Block() as block:
            S.emit(block)
    return nc


_CACHE = {}


def run_layers(inp, x, layers, final, n_cores=None):
    B = x.shape[0]
    wbig, wsm, relb, wgate, sm_off = prep_weights(inp, layers, final)
    nc = build_program(layers, final, wbig.shape[0], wsm.shape[1], relb.shape[0], wgate.shape[0], sm_off)
    in_maps = []
    for b in range(B):
        in_maps.append({"xT": np.ascontiguousarray(x[b].T), "wbig": wbig, "wsm": wsm, "relb": relb, "wgate": wgate})
    res = run_bass_kernel_spmd(nc, in_maps, core_ids=list(range(B)))
    out = np.stack([np.ascontiguousarray(r["outT"].T) for r in res.results], axis=0)
    return out


def kernel(**inputs):
    inp = {k: np.asarray(v, dtype=np.float32) for k, v in inputs.items()}
    x = inp["x"]
    out = run_layers(inp, x, [0, 1, 2, 3], True)
    return out.astype(np.float32)
```
